# Optimizing a Trainium2 kernel written in Bass

```python
import math
import jax, jax.numpy as jnp
from jax import lax
import numpy as np

D_MODEL = 2048
BATCH = 1
SEQ = 8192
DEPTH = 1

HEAD_DIM = 128
N_Q_HEADS = 8
N_KV_HEADS = 2
Q_PER_KV = N_Q_HEADS // N_KV_HEADS
Q_BLOCK = 128
GRID_W = 64
ROPE_THETA = 10000.0
ROPE_PAIRS = HEAD_DIM // 4
GMLP_GROUPS = 4
GMLP_GROUP_DIM = 128
GMLP_WIDTH = GMLP_GROUPS * GMLP_GROUP_DIM
CHUNK = 128
MEM_TOKENS = 256
N_MEM_HEADS = 4
MEM_WIDTH = N_MEM_HEADS * HEAD_DIM
N_BRANCHES = 3
D_FF = 4 * D_MODEL
EPS = 1e-6

ATTN_Q_W = N_Q_HEADS * HEAD_DIM
ATTN_KV_W = N_KV_HEADS * HEAD_DIM
SPLITS = [ATTN_Q_W, ATTN_KV_W, ATTN_KV_W, 2 * GMLP_WIDTH, MEM_WIDTH, N_BRANCHES * D_MODEL]
IN_W = sum(SPLITS)
SPLIT_IDX = list(np.cumsum(SPLITS)[:-1].tolist())

kernel_name = "hybrid_gated_attn_gmlp_memxattn_encoder"


def rms_norm(x, g):
    xf = x.astype(jnp.float32)
    y = xf * lax.rsqrt(jnp.mean(xf * xf, axis=-1, keepdims=True) + EPS)
    return (y * g.astype(jnp.float32)).astype(x.dtype)


def axial_rope_tables(S, dtype):
    rows = S // GRID_W
    t_row = jnp.repeat(jnp.arange(rows), GRID_W).astype(jnp.float32)
    t_col = jnp.tile(jnp.arange(GRID_W), rows).astype(jnp.float32)
    inv = ROPE_THETA ** (-jnp.arange(ROPE_PAIRS, dtype=jnp.float32) / ROPE_PAIRS)
    ar = t_row[:, None] * inv
    ac = t_col[:, None] * inv
    ang = jnp.concatenate([ar, ar, ac, ac], axis=-1)
    return jnp.cos(ang).astype(dtype), jnp.sin(ang).astype(dtype)


def apply_axial_rope(x, cos, sin):
    xs = x.reshape(*x.shape[:-1], 2, 2, ROPE_PAIRS)
    rot = jnp.stack([-xs[..., 1, :], xs[..., 0, :]], axis=-2).reshape(x.shape)
    return x * cos[None, :, None, :] + rot * sin[None, :, None, :]


def self_attention(q, k, v):
    B, S = q.shape[0], q.shape[1]
    nb = S // Q_BLOCK
    q = q * jnp.asarray(HEAD_DIM ** -0.5, q.dtype)
    qb = q.reshape(B, nb, Q_BLOCK, N_KV_HEADS, Q_PER_KV, HEAD_DIM).transpose(1, 0, 2, 3, 4, 5)

    def block(qblk):
        s = jnp.einsum('bqgrd,bkgd->bgrqk', qblk, k).astype(jnp.float32)
        p = jax.nn.softmax(s, axis=-1).astype(v.dtype)
        return jnp.einsum('bgrqk,bkgd->bqgrd', p, v)

    o = lax.map(block, qb)
    return o.transpose(1, 0, 2, 3, 4, 5).reshape(B, S, N_Q_HEADS * HEAD_DIM)


def chunked_gmlp(z, sgu_g, w_s, b_s):
    B, S = z.shape[0], z.shape[1]
    u, v = jnp.split(z, 2, axis=-1)
    v = rms_norm(v, sgu_g)
    vc = v.reshape(B, S // CHUNK, CHUNK, GMLP_GROUPS, GMLP_GROUP_DIM)
    mixed = jnp.einsum('gij,bcjgd->bcigd', w_s, vc) + b_s.T[None, None, :, :, None]
    return u * mixed.reshape(B, S, GMLP_WIDTH)


def memory_attention(qm, mem_n, w_mem_kv, mq_g, mk_g):
    B, S = qm.shape[0], qm.shape[1]
    M = mem_n.shape[1]
    km, vm = jnp.split(mem_n @ w_mem_kv, 2, axis=-1)
    qm = rms_norm(qm.reshape(B, S, N_MEM_HEADS, HEAD_DIM), mq_g) * jnp.asarray(HEAD_DIM ** -0.5, qm.dtype)
    km = rms_norm(km.reshape(B, M, N_MEM_HEADS, HEAD_DIM), mk_g)
    vm = vm.reshape(B, M, N_MEM_HEADS, HEAD_DIM)
    s = jnp.einsum('bshd,bmhd->bhsm', qm, km).astype(jnp.float32)
    p = jax.nn.softmax(s, axis=-1).astype(vm.dtype)
    return jnp.einsum('bhsm,bmhd->bshd', p, vm).reshape(B, S, MEM_WIDTH)


def setup_inputs(seed: int = 0) -> dict:
    key = jax.random.key(seed)
    ks = jax.random.split(key, 24)
    f32 = jnp.float32

    def w(k, shape, fan_in, scale=1.0):
        return jax.random.normal(k, shape, f32) * (scale * fan_in ** -0.5)

    def gain(k, shape):
        return 1.0 + 0.02 * jax.random.normal(k, shape, f32)

    L = DEPTH
    return {
        "x": jax.random.normal(ks[0], (BATCH, SEQ, D_MODEL), f32),
        "mem": jax.random.normal(ks[1], (BATCH, MEM_TOKENS, D_MODEL), f32),
        "norm_mix": gain(ks[2], (L, D_MODEL)),
        "w_in": w(ks[3], (L, D_MODEL, IN_W), D_MODEL),
        "q_norm": gain(ks[4], (L, HEAD_DIM)),
        "k_norm": gain(ks[5], (L, HEAD_DIM)),
        "sgu_norm": gain(ks[6], (L, GMLP_WIDTH)),
        "w_spatial": w(ks[7], (L, GMLP_GROUPS, CHUNK, CHUNK), CHUNK),
        "b_spatial": 1.0 + 0.02 * jax.random.normal(ks[8], (L, GMLP_GROUPS, CHUNK), f32),
        "mem_norm": gain(ks[9], (L, D_MODEL)),
        "w_mem_kv": w(ks[10], (L, D_MODEL, 2 * MEM_WIDTH), D_MODEL),
        "mq_norm": gain(ks[11], (L, HEAD_DIM)),
        "mk_norm": gain(ks[12], (L, HEAD_DIM)),
        "w_attn_o": w(ks[13], (L, ATTN_Q_W, D_MODEL), ATTN_Q_W),
        "w_gmlp_o": w(ks[14], (L, GMLP_WIDTH, D_MODEL), GMLP_WIDTH),
        "w_mem_o": w(ks[15], (L, MEM_WIDTH, D_MODEL), MEM_WIDTH),
        "w_out": w(ks[16], (L, D_MODEL, D_MODEL), D_MODEL, 0.5),
        "norm_ffn": gain(ks[17], (L, D_MODEL)),
        "w_ffn_up": w(ks[18], (L, D_MODEL, D_FF), D_MODEL),
        "w_ffn_down": w(ks[19], (L, D_FF, D_MODEL), D_FF, 0.5),
    }


def reference(x, mem, norm_mix, w_in, q_norm, k_norm, sgu_norm, w_spatial, b_spatial,
              mem_norm, w_mem_kv, mq_norm, mk_norm, w_attn_o, w_gmlp_o, w_mem_o, w_out,
              norm_ffn, w_ffn_up, w_ffn_down):
    B, S = x.shape[0], x.shape[1]
    cos, sin = axial_rope_tables(S, x.dtype)
    for l in range(DEPTH):
        h = rms_norm(x, norm_mix[l])
        proj = h @ w_in[l]
        q, k, v, zg, qm, gate_logits = jnp.split(proj, SPLIT_IDX, axis=-1)

        q = apply_axial_rope(rms_norm(q.reshape(B, S, N_Q_HEADS, HEAD_DIM), q_norm[l]), cos, sin)
        k = apply_axial_rope(rms_norm(k.reshape(B, S, N_KV_HEADS, HEAD_DIM), k_norm[l]), cos, sin)
        v = v.reshape(B, S, N_KV_HEADS, HEAD_DIM)
        y_attn = self_attention(q, k, v) @ w_attn_o[l]

        y_gmlp = chunked_gmlp(jax.nn.gelu(zg), sgu_norm[l], w_spatial[l], b_spatial[l]) @ w_gmlp_o[l]

        mem_n = rms_norm(mem, mem_norm[l])
        y_mem = memory_attention(qm, mem_n, w_mem_kv[l], mq_norm[l], mk_norm[l]) @ w_mem_o[l]

        g_a, g_g, g_m = jnp.split(jax.nn.sigmoid(gate_logits), N_BRANCHES, axis=-1)
        merged = g_a * y_attn + g_g * y_gmlp + g_m * y_mem
        x = x + merged @ w_out[l]

        h2 = rms_norm(x, norm_ffn[l])
        x = x + jnp.square(jax.nn.relu(h2 @ w_ffn_up[l])) @ w_ffn_down[l]
    return x
```

```python
import contextlib
import math
import numpy as np
import concourse.bass as bass
import concourse.mybir as mybir
from concourse.bass_utils import run_bass_kernel_spmd

F32 = mybir.dt.float32
BF16 = mybir.dt.bfloat16
AF = mybir.ActivationFunctionType
ALU = mybir.AluOpType
AX = mybir.AxisListType

NCORES = 8
S = 8192
D = 2048
TOK = S // NCORES
NCH = D // 128
EPS = 1e-6
SCALE = 128.0 ** -0.5
GRAN = 256
ARENA_KB = 196
ENGS = ("pe", "act", "dve", "pool", "sp")


class Ref:
    __slots__ = ("ap", "gran")

    def __init__(self, ap, gran):
        self.ap = ap
        self.gran = gran

    def v(self, f):
        return Ref(f(self.ap), self.gran)


class Buf:
    def __init__(self, arena, off, shape, dtype):
        isz = 4 if dtype == F32 else 2
        n = int(np.prod(shape))
        assert off % 4 == 0 and off + n * isz <= ARENA_KB * 1024, (off, shape)
        a = arena[:, off // 2: off // 2 + n * isz // 2]
        if dtype == F32:
            a = a.bitcast(F32)
        if len(shape) == 2:
            a = a.rearrange("p (a b) -> p a b", b=shape[1])
        elif len(shape) == 3:
            a = a.rearrange("p (a b c) -> p a b c", b=shape[1], c=shape[2])
        self.ap = a
        self.isz = isz
        self.idx = np.arange(n, dtype=np.int64).reshape(shape) * isz + off
        self.nd = len(shape)

    def __getitem__(self, key):
        if not isinstance(key, tuple):
            key = (key,)
        ap = self.ap[(slice(None),) + key]
        sub = np.asarray(self.idx[key]).ravel()
        lo = int(sub.min()) // GRAN
        g = np.unique(np.concatenate([sub // GRAN, (sub + self.isz - 1) // GRAN]))
        return Ref(ap, frozenset(g.tolist()))

    def all(self):
        return self[(slice(None),) * self.nd]


class Prog:
    def __init__(self, nc, same_engine_sync=("act", "dve", "pool")):
        self.nc = nc
        self.ins = []
        self.last_writer = {}
        self.readers = {}
        self.same_engine_sync = set(same_engine_sync)
        self.dma_slots = {}

    def add(self, eng, fn, reads=(), writes=(), dma_slot=None):
        idx = len(self.ins)
        deps = set()
        rg = set()
        for r in reads:
            rg |= r.gran
        wg = set()
        for r in writes:
            wg |= r.gran
        lw = self.last_writer
        rd = self.readers
        for g in rg:
            w = lw.get(g)
            if w is not None:
                deps.add(w)
        for g in wg:
            w = lw.get(g)
            if w is not None:
                deps.add(w)
            d = rd.get(g)
            if d:
                deps.update(d.values())
        deps.discard(idx)
        rec = dict(eng=eng, fn=fn, deps=deps, dma_slot=dma_slot, sig=None, needs_sig=False)
        if dma_slot is not None:
            n = self.dma_slots.get(dma_slot, 0) + 1
            self.dma_slots[dma_slot] = n
            rec["sig"] = (("dma", dma_slot), 16 * n)
        self.ins.append(rec)
        rkey = eng if dma_slot is None else ("dma", idx)
        for g in rg:
            rd.setdefault(g, {})[rkey] = idx
        for g in wg:
            lw[g] = idx
            rd[g] = {}
        return idx

    def emit(self):
        nc = self.nc
        ins = self.ins
        ses = self.same_engine_sync
        for r in ins:
            for d in r["deps"]:
                p = ins[d]
                if p["dma_slot"] is not None:
                    continue
                if p["eng"] != r["eng"] or r["dma_slot"] is not None or p["eng"] in ses:
                    p["needs_sig"] = True
        cnt = {e: 0 for e in ENGS}
        for r in ins:
            if r["dma_slot"] is None and r["needs_sig"]:
                cnt[r["eng"]] += 1
                r["sig"] = (("eng", r["eng"]), cnt[r["eng"]])
        stack = contextlib.ExitStack()
        sems = {}
        for e in ENGS:
            sems[("eng", e)] = stack.enter_context(nc.semaphore(f"s_{e}"))
        for s in self.dma_slots:
            sems[("dma", s)] = stack.enter_context(nc.semaphore(f"d_{s}"))
        per_eng = {e: [] for e in ENGS}
        for i, r in enumerate(ins):
            per_eng[r["eng"]].append(i)
        block = stack.enter_context(nc.Block())
        nwaits = [0]

        def run(eng_name, engine):
            waited = {}
            for i in per_eng[eng_name]:
                r = ins[i]
                need = {}
                for d in r["deps"]:
                    p = ins[d]
                    if p["sig"] is None:
                        continue
                    if (p["dma_slot"] is None and p["eng"] == eng_name and r["dma_slot"] is None
                            and eng_name not in ses):
                        continue
                    k, v = p["sig"]
                    if need.get(k, 0) < v:
                        need[k] = v
                for k, v in need.items():
                    if waited.get(k, 0) >= v:
                        continue
                    engine.wait_ge(sems[k], v)
                    nwaits[0] += 1
                    waited[k] = v
                inst = r["fn"](engine)
                if r["dma_slot"] is not None:
                    inst.then_inc(sems[("dma", r["dma_slot"])], 16)
                elif r["needs_sig"]:
                    inst.then_inc(sems[("eng", eng_name)], 1)
            if eng_name == "sp":
                for s, n in self.dma_slots.items():
                    k = ("dma", s)
                    if waited.get(k, 0) < 16 * n:
                        engine.wait_ge(sems[k], 16 * n)
                for e in ENGS:
                    if e != "sp" and cnt[e] > 0:
                        engine.wait_ge(sems[("eng", e)], cnt[e])

        @block.tensor
        def _(e):
            run("pe", e)

        @block.scalar
        def _(e):
            run("act", e)

        @block.vector
        def _(e):
            run("dve", e)

        @block.gpsimd
        def _(e):
            run("pool", e)

        @block.sync
        def _(e):
            run("sp", e)

        stack.close()
        self.stats = dict(n_ins=len(ins), sig=cnt, waits=nwaits[0],
                          per_eng={e: len(v) for e, v in per_eng.items()})


def build(stop_after=None, debug=()):
    nc = bass.Bass("TRN2", target_bir_lowering=False)
    st = contextlib.ExitStack()

    def din(name, shape, dt=F32):
        return nc.dram_tensor(name, list(shape), dt, kind="ExternalInput").ap()

    xT_all = din("xT_all", [16, 128, 16, 512])
    xT_own = din("xT_own", [2, 128, 16, 512])
    cos_k = din("cos_k", [16, 128, 4, 128])
    sin_k = din("sin_k", [16, 128, 4, 128])
    cos_q = din("cos_q", [128, 8, 128])
    sin_q = din("sin_q", [128, 8, 128])
    ident_d = din("ident", [128, 128])
    gvec_d = din("gvec", [128, 48])
    grow_d = din("grow", [128, 4, 128])
    sgu_d = din("sgu", [128, 512])
    bsp_d = din("bsp", [128, 512])
    wsT_d = din("wsT", [128, 512])
    memT_d = din("memT", [128, 16, 256])
    wmkv_d = din("wmkv", [128, 16, 1024])
    wkv_d = din("wkv", [128, 16, 512])
    wq_d = din("wq", [128, 16, 1024])
    wqm_d = din("wqm", [128, 16, 512])
    wzu_d = din("wzu", [128, 16, 512])
    wzv_d = din("wzv", [128, 16, 512])
    wp5_d = din("wp5", [16, 128, 4, 2048])
    wout_d = din("wout", [4, 128, 4, 2048])
    wup_d = din("wup", [16, 128, 4, 2048])
    wdn_d = din("wdn", [16, 128, 4, 2048])
    outT = nc.dram_tensor("outT", [128, 16, TOK], F32, kind="ExternalOutput").ap()
    dbg = {}
    dbg_shapes = dict(KT=([128, 2, 8192], BF16), V=([128, 64, 256], BF16), hT=([128, 16, 1024], BF16),
                      QT=([128, 8, 1024], BF16), attnT=([128, 8, 1024], BF16), memoT=([128, 4, 1024], BF16),
                      gmT=([128, 4, 1024], BF16), mergedT=([128, 16, 1024], BF16), kmT=([128, 4, 256], BF16),
                      vm=([128, 2, 512], BF16), x1=([128, 16, 1024], F32))
    for k in debug:
        shp, dt = dbg_shapes[k]
        dbg[k] = nc.dram_tensor("dbg_" + k, shp, dt, kind="ExternalOutput").ap()

    arena = st.enter_context(nc.sbuf_tensor("arena", [128, ARENA_KB * 512], BF16))
    banks = [st.enter_context(nc.psum_tensor(f"bank{i}", [128, 512], F32)) for i in range(8)]

    def PS(i, cols=512, dt=F32):
        b = banks[i]
        if dt == BF16:
            ap = b.bitcast(BF16)[:, 0:cols]
        else:
            ap = b[:, 0:cols]
        return Ref(ap, frozenset([("ps", i)]))

    KB = 1024
    P = Prog(nc)

    def B(off_kb, shape, dt):
        return Buf(arena, int(off_kb * KB), shape, dt)

    ident = B(0, [128], BF16)
    ones = B(0.25, [128], BF16)
    gvec = B(0.5, [48], F32)
    small = B(0.75, [64], F32)
    grow = B(1, [4, 128], F32)
    sgu = B(3, [512], F32)
    bsp = B(5, [512], F32)
    wsT = B(7, [512], BF16)
    kmT = B(8, [4, 256], BF16)
    vm = B(10, [2, 512], BF16)
    hT = B(12, [16, 1024], BF16)
    xres = B(12, [16, 1024], F32)
    KT = B(44, [2, 8192], BF16)
    V = B(76, [64, 256], BF16)
    mergedT = B(76, [16, 1024], BF16)
    h2T = B(76, [16, 1024], BF16)
    attnT = B(108, [8, 1024], BF16)
    memoT = B(124, [4, 1024], BF16)
    gmT = B(132, [4, 1024], BF16)
    hid = B(108, [16, 1024], BF16)
    ring = [B(140 + 16 * i, [16, 512], BF16) for i in range(3)]

    def sp_dma(out_ref, in_ap, slot):
        P.add("sp", lambda e: e.dma_start(out=out_ref.ap, in_=in_ap), writes=[out_ref], dma_slot=slot)

    def pool_dma(out_ref, in_ap, slot):
        P.add("pool", lambda e: e.dma_start(out=out_ref.ap, in_=in_ap), writes=[out_ref], dma_slot=slot)

    def MM(out, lhsT, rhs, start, stop):
        P.add("pe", lambda e: e.matmul(out.ap, lhsT=lhsT.ap, rhs=rhs.ap, start=start, stop=stop),
              reads=[lhsT, rhs], writes=[out])

    def TR(out, in_):
        P.add("pe", lambda e: e.transpose(out=out.ap, in_=in_.ap, identity=ident.all().ap),
              reads=[in_, ident.all()], writes=[out])

    def ACT(out, in_, func, scale=None, bias=None, extra_reads=()):
        kw = {}
        if scale is not None:
            kw["scale"] = scale if not isinstance(scale, Ref) else scale.ap
        if bias is not None:
            kw["bias"] = bias if not isinstance(bias, Ref) else bias.ap
        rds = [in_] + [x for x in (scale, bias) if isinstance(x, Ref)] + list(extra_reads)
        P.add("act", lambda e: e.activation(out=out.ap, in_=in_.ap, func=func, **kw), reads=rds, writes=[out])

    def TT(out, in0, in1, op, eng="dve"):
        P.add(eng, lambda e: e.tensor_tensor(out=out.ap, in0=in0.ap, in1=in1.ap, op=op),
              reads=[in0, in1], writes=[out])

    def TS(out, in0, s1, op0, s2=None, op1=None, eng="dve"):
        rds = [in0] + [x for x in (s1, s2) if isinstance(x, Ref)]
        a1 = s1.ap if isinstance(s1, Ref) else s1
        a2 = s2.ap if isinstance(s2, Ref) else s2
        if op1 is None:
            P.add(eng, lambda e: e.tensor_scalar(out=out.ap, in0=in0.ap, scalar1=a1, scalar2=None, op0=op0),
                  reads=rds, writes=[out])
        else:
            P.add(eng, lambda e: e.tensor_scalar(out=out.ap, in0=in0.ap, scalar1=a1, scalar2=a2, op0=op0, op1=op1),
                  reads=rds, writes=[out])

    def STT(out, in0, scalar, in1, op0, op1):
        rds = [in0, in1] + ([scalar] if isinstance(scalar, Ref) else [])
        sc = scalar.ap if isinstance(scalar, Ref) else scalar
        P.add("dve", lambda e: e.scalar_tensor_tensor(out=out.ap, in0=in0.ap, scalar=sc, in1=in1.ap, op0=op0, op1=op1),
              reads=rds, writes=[out])

    def RECIP(out, in_):
        P.add("dve", lambda e: e.reciprocal(out=out.ap, in_=in_.ap), reads=[in_], writes=[out])

    def RSUM(out, in_):
        P.add("dve", lambda e: e.reduce_sum(out=out.ap, in_=in_.ap, axis=AX.X), reads=[in_], writes=[out])

    def COPY(out, in_, eng="dve"):
        if eng == "act":
            ACT(out, in_, AF.Copy)
        else:
            P.add(eng, lambda e: e.tensor_copy(out=out.ap, in_=in_.ap), reads=[in_], writes=[out])

    def dump(name, ref):
        if name in dbg:
            P.add("sp", lambda e: e.dma_start(out=dbg[name], in_=ref.ap), reads=[ref], dma_slot="dbg_" + name)

    def bc(ap, shape):
        return ap.broadcast_to(shape)

    def qk_pipeline(src, G, nh, gain, cos, sin, scr, out):
        if G >= 2:
            gs_, hs_ = [slice(0, G // 2), slice(G // 2, G)], [slice(0, nh)] * 2
            Gg, nhg = G // 2, nh
        else:
            gs_, hs_ = [slice(0, 1)] * 2, [slice(0, nh // 2), slice(nh // 2, nh)]
            Gg, nhg = G, nh // 2
        g4 = lambda ap: ap.rearrange("p g (h d) -> p g h d", d=128)
        s3 = lambda ap: ap.rearrange("p (g h) -> p g h", h=nh)

        def big(ref, i):
            return ref.v(lambda ap: g4(ap)[:, gs_[i], hs_[i], :])

        def sm(ref, i):
            return ref.v(lambda ap: s3(ap)[:, gs_[i], hs_[i]])

        def tab(ref, i):
            return ref.v(lambda ap: ap[:, gs_[i], :])

        sq, qn, a_, b_ = scr["sq"].all(), scr["qn"].all(), scr["a"].all(), scr["b"].all()
        ss, sd, rs = scr["ss"].all(), scr["sd"].all(), scr["rs"].all()
        R2 = range(2)
        for i in R2:
            ACT(big(sq, i), big(src, i), AF.Square)
        for i in R2:
            RSUM(sm(ss, i), big(sq, i))
        for i in R2:
            ACT(sm(sd, i), sm(ss, i), AF.Sqrt, scale=1.0 / 128.0, bias=eps_col)
        for i in R2:
            RECIP(sm(rs, i), sm(sd, i))
        for i in R2:
            TT(big(qn, i), big(src, i), sm(rs, i).v(lambda ap: bc(ap.unsqueeze(3), [128, Gg, nhg, 128])), ALU.mult)
        dst = qn if cos is not None else out
        for i in R2:
            TT(big(dst, i), big(qn, i), gain.v(lambda ap: bc(ap.unsqueeze(1).unsqueeze(1), [128, Gg, nhg, 128])), ALU.mult)
        if cos is None:
            return
        for i in R2:
            TT(big(a_, i), big(qn, i), tab(cos, i).v(lambda ap: bc(ap.unsqueeze(2), [128, Gg, nhg, 128])), ALU.mult)
        r5 = lambda ap: ap.rearrange("p g h (f r i) -> p g h f r i", f=2, r=2, i=32)
        t5 = lambda ap: ap.rearrange("p g (f r i) -> p g f r i", f=2, r=2, i=32)
        for r in range(2):
            for i in R2:
                if Gg == 1:
                    TT(big(b_, i).v(lambda ap: r5(ap)[:, 0, :, :, r, :]), big(qn, i).v(lambda ap: r5(ap)[:, 0, :, :, 1 - r, :]),
                       tab(sin, i).v(lambda ap: bc(t5(ap)[:, :, :, r, :], [128, nhg, 2, 32])), ALU.mult)
                else:
                    for h in range(nhg):
                        TT(big(b_, i).v(lambda ap: r5(ap)[:, :, h, :, r, :]), big(qn, i).v(lambda ap: r5(ap)[:, :, h, :, 1 - r, :]),
                           tab(sin, i).v(lambda ap: t5(ap)[:, :, :, r, :]), ALU.mult)
        for i in R2:
            TT(big(out, i), big(a_, i), big(b_, i), ALU.add)

    eps_col = small[0:1]
    one_f = small[1:2]
    P.add("dve", lambda e: e.memset(eps_col.ap, EPS), writes=[eps_col])
    P.add("dve", lambda e: e.memset(one_f.ap, 1.0), writes=[one_f])
    P.add("dve", lambda e: e.memset(ones.all().ap, 1.0), writes=[ones.all()])
    pool_dma(ident.all(), ident_d, "c_ident")
    sp_dma(gvec.all(), gvec_d, "c_gvec")
    sp_dma(grow.all(), grow_d, "c_grow")
    sp_dma(sgu.all(), sgu_d, "c_sgu")
    sp_dma(bsp.all(), bsp_d, "c_bsp")
    pool_dma(wsT.all(), wsT_d, "c_wsT")
    g_mix = lambda c: gvec[c:c + 1]
    g_mem = lambda c: gvec[16 + c:17 + c]
    g_ffn = lambda c: gvec[32 + c:33 + c]
    g_q, g_k, g_mq, g_mk = grow[0], grow[1], grow[2], grow[3]

    wmkv = B(12, [16, 1024], BF16)
    memT = B(44, [16, 256], F32)
    memn = B(60, [16, 256], BF16)
    msq = B(68, [16, 256], BF16)
    mrs = B(108, [256], F32)
    mrs2 = B(109, [256], F32)
    pool_dma(wmkv.all(), wmkv_d, "w_mkv")
    sp_dma(memT.all(), memT_d, "x_mem")
    for c4 in range(4):
        ACT(msq[4 * c4:4 * c4 + 4], memT[4 * c4:4 * c4 + 4], AF.Square)
    for c in range(16):
        MM(PS(0, 256), ones.all(), msq[c], c == 0, c == 15)
    ACT(mrs.all(), PS(0, 256), AF.Sqrt, scale=1.0 / D, bias=eps_col)
    RECIP(mrs2.all(), mrs.all())
    for c in range(16):
        STT(memn[c], memT[c], g_mem(c), mrs2.all(), ALU.mult, ALU.mult)
    mscr = dict(sq=B(110, [1, 512], F32), qn=B(112, [1, 512], F32), a=B(110, [1, 512], F32), b=B(110, [1, 512], F32),
                ss=B(114, [4], F32), sd=B(114.25, [4], F32), rs=B(114.5, [4], F32))
    kmn = B(115, [1, 512], BF16)
    for mt in range(2):
        for half in range(2):
            for c in range(16):
                MM(PS(1 + half), memn[c, mt * 128:(mt + 1) * 128], wmkv[c, half * 512:(half + 1) * 512], c == 0, c == 15)
        qk_pipeline(PS(1).v(lambda ap: ap.unsqueeze(1)), 1, 4, g_mk, None, None, mscr, kmn.all())
        for h in range(4):
            TR(Ref(PS(3, 1024, BF16).ap[:, h * 128:(h + 1) * 128], PS(3).gran), kmn[0, h * 128:(h + 1) * 128])
        COPY(kmT[:, mt * 128:(mt + 1) * 128],
             PS(3, 512, BF16).v(lambda ap: ap.rearrange("p (h m) -> p h m", m=128)))
        COPY(vm[mt], PS(2), eng="act")
    dump("kmT", kmT.all())
    dump("vm", vm.all())

    wkv_f = B(76, [16, 512], F32)
    wkv_g = B(108, [16, 512], BF16)
    sp_dma(wkv_f.all(), wkv_d, "w_kv")
    for c in range(16):
        TS(wkv_g[c], wkv_f[c], g_mix(c), ALU.mult)

    xs = [B(140, [16, 512], BF16), B(156, [16, 512], BF16)]
    sqs = B(172, [16, 512], BF16)
    tabs = [(B(124, [4, 128], F32), B(126, [4, 128], F32)), (B(128, [4, 128], F32), B(130, [4, 128], F32))]
    kv_sb = [B(12, [4, 512], F32), B(20, [4, 512], F32)]
    kscr = dict(sq=B(28, [4, 256], F32), qn=B(32, [4, 256], F32), a=B(36, [4, 256], F32), b=B(40, [4, 256], F32),
                ss=B(138, [8], F32), sd=B(138.125, [8], F32), rs=B(138.25, [8], F32))
    krope = [B(132, [4, 256], BF16), B(134, [4, 256], BF16)]
    sd_bc = B(136, [512], F32)
    rtok = [B(138.5, [4], F32), B(138.75, [4], F32)]
    KVB = [2, 3, 6, 7]
    NT1 = 16

    def p1_load(T):
        b = T % 2
        pool_dma(xs[b].all(), xT_all[T], f"xs{b}")

    def p1_tabs(T):
        b = T % 2
        sp_dma(tabs[b][0].all(), cos_k[T], f"tc{b}")
        sp_dma(tabs[b][1].all(), sin_k[T], f"ts{b}")

    def p1_squares(T):
        b = T % 2
        for c4 in range(4):
            if c4 < 3:
                ACT(sqs[4 * c4:4 * c4 + 4], xs[b][4 * c4:4 * c4 + 4], AF.Square)
            else:
                TT(sqs[4 * c4:4 * c4 + 4], xs[b][4 * c4:4 * c4 + 4], xs[b][4 * c4:4 * c4 + 4], ALU.mult)

    def p1_stats(T):
        b = T % 2
        for c in range(16):
            MM(PS(0), ones.all(), sqs[c], c == 0, c == 15)
        ACT(sd_bc.all(), PS(0), AF.Sqrt, scale=1.0 / D, bias=eps_col)
        for sub in range(4):
            MM(Ref(banks[1][:, sub:sub + 1], PS(1).gran),
               sd_bc.all().v(lambda ap: ap[0:1, sub * 128:(sub + 1) * 128]), one_f.v(lambda ap: ap[0:1, :]), True, True)
        RECIP(rtok[b].all(), Ref(banks[1][:, 0:4], PS(1).gran))

    def p1_B(T):
        b = T % 2
        for sub in range(4):
            pb = KVB[sub]
            for c in range(16):
                MM(PS(pb), xs[b][c, sub * 128:(sub + 1) * 128], wkv_g[c], c == 0, c == 15)
            ACT(kv_sb[b][sub], PS(pb), AF.Copy, scale=rtok[b][sub:sub + 1])

    def p1_C(T):
        b = T % 2
        COPY(V[T * 4:T * 4 + 4], kv_sb[b][:, 256:512], eng="act")
        qk_pipeline(kv_sb[b][:, 0:256], 4, 2, g_k, tabs[b][0].all(), tabs[b][1].all(), kscr, krope[b].all())

    def p1_D(T):
        b = T % 2
        pt_b = 4 + b
        for sub in range(4):
            for h in range(2):
                j = sub * 2 + h
                TR(Ref(PS(pt_b, 1024, BF16).ap[:, j * 128:(j + 1) * 128], PS(pt_b).gran), krope[b][sub, h * 128:(h + 1) * 128])
        COPY(KT[:, T * 512:(T + 1) * 512].v(lambda ap: ap.rearrange("p h (s j) -> p h s j", j=128)),
             PS(pt_b, 1024, BF16).v(lambda ap: ap.rearrange("p (s h j) -> p h s j", h=2, j=128)), eng="dve")

    p1_load(0)
    p1_tabs(0)
    p1_squares(0)
    for T in range(NT1 + 1):
        if T + 1 < NT1:
            p1_load(T + 1)
        if T < NT1:
            p1_stats(T)
        if T >= 1:
            p1_C(T - 1)
        if T + 1 < NT1:
            p1_tabs(T + 1)
            p1_squares(T + 1)
        if T < NT1:
            p1_B(T)
        if T >= 1:
            p1_D(T - 1)
    dump("KT", KT.all())
    dump("V", V.all())

    wq = B(108, [16, 1024], BF16)
    pool_dma(wq.all(), wq_d, "w_q")
    xo = [B(140, [16, 256], F32), B(156, [16, 256], F32)]
    sqo = [B(172, [16, 256], BF16), B(180, [16, 256], BF16)]
    rbc = [B(188, [256], F32), B(189, [256], F32)]
    rbc2 = [B(190, [256], F32), B(191, [256], F32)]
    for hh in range(4):
        t, o = hh // 2, (hh % 2) * 256
        b = hh % 2
        sp_dma(xo[b].all(), xT_own[t][:, :, o:o + 256], f"xo{b}")
        for c4 in range(4):
            ACT(sqo[b][4 * c4:4 * c4 + 4], xo[b][4 * c4:4 * c4 + 4], AF.Square)
        for c in range(16):
            MM(PS(b, 256), ones.all(), sqo[b][c], c == 0, c == 15)
        ACT(rbc[b].all(), PS(b, 256), AF.Sqrt, scale=1.0 / D, bias=eps_col)
        RECIP(rbc2[b].all(), rbc[b].all())
        for c in range(16):
            STT(hT[c, t * 512 + o:t * 512 + o + 256], xo[b][c], g_mix(c), rbc2[b].all(), ALU.mult, ALU.mult)
    dump("hT", hT.all())

    QT = B(172, [8, 1024], BF16)
    tq = (B(188, [8, 128], F32), B(192, [8, 128], F32))
    sp_dma(tq[0].all(), cos_q, "tqc")
    sp_dma(tq[1].all(), sin_q, "tqs")
    qscr = dict(sq=B(140, [1, 1024], F32), qn=B(144, [1, 1024], F32), a=B(148, [1, 1024], F32), b=B(152, [1, 1024], F32),
                ss=B(168, [8], F32), sd=B(168.25, [8], F32), rs=B(168.5, [8], F32))
    qrope = [B(156, [1, 1024], BF16), B(158, [1, 1024], BF16)]
    qsb = [B(160, [1, 1024], F32), B(164, [1, 1024], F32)]

    def p2_mm(tt):
        b = tt % 2
        for half in range(2):
            pb = 2 * b + half
            for c in range(16):
                MM(PS(pb), hT[c, tt * 128:(tt + 1) * 128], wq[c, half * 512:(half + 1) * 512], c == 0, c == 15)

    gsw = B(169, [128], F32)
    for blk, src_blk in ((0, 1), (1, 0), (2, 3), (3, 2)):
        COPY(gsw[blk * 32:(blk + 1) * 32], g_q.v(lambda ap: ap[:, src_blk * 32:(src_blk + 1) * 32]), eng="act")
    TT(tq[0].all(), tq[0].all(), g_q.v(lambda ap: bc(ap.unsqueeze(1), [128, 8, 128])), ALU.mult)
    TT(tq[1].all(), tq[1].all(), gsw.all().v(lambda ap: bc(ap.unsqueeze(1), [128, 8, 128])), ALU.mult)

    def p2_pipe(tt):
        b = tt % 2
        sq, qn, a_, b_ = qscr["sq"], qscr["qn"], qscr["a"], qscr["b"]
        ss, sd, rs = qscr["ss"], qscr["sd"], qscr["rs"]
        R2 = range(2)
        hsl = lambda i: slice(i * 512, (i + 1) * 512)
        h4 = lambda ap: ap.rearrange("p (h d) -> p h d", d=128)
        r5 = lambda ap: ap.rearrange("p (h f r i) -> p h f r i", f=2, r=2, i=32)
        for i in R2:
            ACT(sq[0, hsl(i)], PS(2 * b + i), AF.Square)
        for i in R2:
            RSUM(ss[4 * i:4 * i + 4], sq[0, hsl(i)].v(h4))
        for i in R2:
            ACT(sd[4 * i:4 * i + 4], ss[4 * i:4 * i + 4], AF.Sqrt, scale=1.0 / 128.0, bias=eps_col)
        for i in R2:
            RECIP(rs[4 * i:4 * i + 4], sd[4 * i:4 * i + 4])
        for h in range(4):
            for i in R2:
                j = 4 * i + h
                ACT(qn[0, j * 128:(j + 1) * 128], PS(2 * b + i).v(lambda ap: ap[:, h * 128:(h + 1) * 128]), AF.Copy,
                    scale=rs[j:j + 1])
        for i in R2:
            TT(a_[0, hsl(i)].v(h4), qn[0, hsl(i)].v(h4), tq[0][tt].v(lambda ap: bc(ap.unsqueeze(1), [128, 4, 128])), ALU.mult)
        for r in range(2):
            for i in R2:
                TT(b_[0, hsl(i)].v(lambda ap: r5(ap)[:, :, :, r, :]), qn[0, hsl(i)].v(lambda ap: r5(ap)[:, :, :, 1 - r, :]),
                   tq[1][tt].v(lambda ap: bc(ap.rearrange("p (f r i) -> p f r i", f=2, r=2, i=32)[:, :, r, :].unsqueeze(1), [128, 4, 2, 32])),
                   ALU.mult)
        for i in R2:
            TT(qrope[b][0, hsl(i)], a_[0, hsl(i)], b_[0, hsl(i)], ALU.add)

    def p2_tr(tt):
        b = tt % 2
        pb = 4 + b
        for h in range(8):
            TR(Ref(PS(pb, 1024, BF16).ap[:, h * 128:(h + 1) * 128], PS(pb).gran), qrope[b][0, h * 128:(h + 1) * 128])
        COPY(QT[:, tt * 128:(tt + 1) * 128], PS(pb, 1024, BF16).v(lambda ap: ap.rearrange("p (h j) -> p h j", j=128)),
             eng="dve")

    p2_mm(0)
    for tt in range(8):
        if tt + 1 < 8:
            p2_mm(tt + 1)
        p2_pipe(tt)
        p2_tr(tt)
    dump("QT", QT.all())

    pt = [B(132 + i, [512], BF16) for i in range(4)]
    rl = B(136, [512], F32)

    accL = [B(124, [512], F32), B(126, [512], F32)]
    ones_f = B(138, [128], F32)

    def attention(n_kt, qT_of, kT_of, v_of, out_of, units, sbank, obank, lbank, half_l=False):
        ahead = len(sbank) - 1
        deferred = []

        def S(u, q, kt):
            MM(PS(sbank[kt % len(sbank)]), kT_of(u, kt), q, True, True)

        for ui, u in enumerate(units):
            ob = obank[ui % len(obank)]
            lb = lbank[ui % len(lbank)]
            acc = accL[ui % 2].all()
            q = qT_of(u)
            if ui == 0:
                for k0 in range(min(ahead, n_kt)):
                    S(u, q, k0)
            defer_now = n_kt >= 32
            for kt in range(n_kt):
                pbuf = pt[kt % len(pt)].all()
                ACT(pbuf, PS(sbank[kt % len(sbank)]), AF.Exp, scale=SCALE)
                if kt + ahead < n_kt:
                    S(u, q, kt + ahead)
                MM(PS(ob), v_of(u, kt), pbuf, kt == 0, kt == n_kt - 1)
                if half_l and kt % 2 == 1:
                    if kt == 1:
                        COPY(acc, pbuf, eng="dve")
                    else:
                        TT(acc, acc, pbuf, ALU.add)
                else:
                    MM(PS(lb), ones.all(), pbuf, kt == 0, (kt == n_kt - 1) and not half_l)
                while deferred and deferred[0][0] <= kt:
                    deferred.pop(0)[1]()
            if ui + 1 < len(units):
                un = units[ui + 1]
                qn_ = qT_of(un)
                for k0 in range(min(ahead, n_kt)):
                    S(un, qn_, k0)
            if half_l:
                MM(PS(lb), ones_f.all(), acc, False, True)

            def mk_epilogue(ob=ob, lb=lb, u=u):
                fns = []
                for pc in range(4):
                    csl = slice(pc * 128, (pc + 1) * 128)
                    fns.append(lambda csl=csl: RECIP(rl[csl], PS(lb).v(lambda ap: ap[:, csl])))
                fns.append(lambda: TT(out_of(u), PS(ob), rl.all(), ALU.mult))
                return fns

            fns = mk_epilogue()
            if defer_now and ui + 1 < len(units):
                deferred = [(6 + 4 * i, f) for i, f in enumerate(fns)]
            else:
                for f in fns:
                    f()
        for _, f in deferred:
            f()

    P.add("dve", lambda e: e.memset(ones_f.all().ap, 1.0), writes=[ones_f.all()])
    units = [(h, qt) for qt in range(2) for h in range(8)]
    attention(64,
              lambda u: QT[u[0], u[1] * 512:(u[1] + 1) * 512],
              lambda u, kt: KT[u[0] // 4, kt * 128:(kt + 1) * 128],
              lambda u, kt: V[kt, (u[0] // 4) * 128:(u[0] // 4 + 1) * 128],
              lambda u: attnT[u[0], u[1] * 512:(u[1] + 1) * 512],
              units, [0, 1, 2, 7], [3, 4], [5, 6], half_l=True)
    dump("attnT", attnT.all())

    wqm = B(140, [16, 512], BF16)
    qmT = B(188, [4, 1024], BF16)
    wzu = B(156, [16, 512], BF16)
    pool_dma(wqm.all(), wqm_d, "w_qm")
    pool_dma(wzu.all(), wzu_d, "w_zu")
    m3 = dict(sq=B(44, [1, 512], F32), qn=B(46, [1, 512], F32), a=B(44, [1, 512], F32), b=B(44, [1, 512], F32),
              ss=B(48, [4], F32), sd=B(48.25, [4], F32), rs=B(48.5, [4], F32))
    qmn = [B(49, [1, 512], BF16), B(50, [1, 512], BF16)]

    def p3_mm(tt):
        pb = 6 + tt % 2
        for c in range(16):
            MM(PS(pb), hT[c, tt * 128:(tt + 1) * 128], wqm[c], c == 0, c == 15)

    def p3_post(tt):
        pb = 6 + tt % 2
        b = tt % 2
        qk_pipeline(PS(pb).v(lambda ap: ap.unsqueeze(1)), 1, 4, g_mq, None, None, m3, qmn[b].all())
        pb2 = 3 + b
        for h in range(4):
            TR(Ref(PS(pb2, 1024, BF16).ap[:, h * 128:(h + 1) * 128], PS(pb2).gran), qmn[b][0, h * 128:(h + 1) * 128])
        COPY(qmT[:, tt * 128:(tt + 1) * 128], PS(pb2, 512, BF16).v(lambda ap: ap.rearrange("p (h j) -> p h j", j=128)),
             eng="dve")

    for tt in range(9):
        if tt < 8:
            p3_mm(tt)
        if tt >= 1:
            p3_post(tt - 1)

    wzv = B(140, [16, 512], BF16)
    uT = B(76, [4, 1024], F32)
    pool_dma(wzv.all(), wzv_d, "w_zv")
    C1 = 0.044715
    C2 = 2.0 * math.sqrt(2.0 / math.pi)
    gt = [(B(52 + 4 * i, [512], F32), B(54 + 4 * i, [512], F32)) for i in range(2)]

    def gelu(out, src_ps, i):
        t1, t2 = gt[i % 2][0].all(), gt[i % 2][1].all()
        ACT(t1, src_ps, AF.Square)
        TS(t1, t1, C1, ALU.mult, 1.0, ALU.add)
        TT(t1, t1, src_ps, ALU.mult)
        ACT(t2, t1, AF.Sigmoid, scale=C2)
        TT(out, t2, src_ps, ALU.mult)

    mem_units = [(h, qt) for qt in range(2) for h in range(4)]
    UB = (2, 7)

    def mem_front(i):
        h, qt = mem_units[i]
        q = qmT[h, qt * 512:(qt + 1) * 512]
        ob, lb = (5, 6)[i % 2], (3, 4)[i % 2]
        for kt in range(2):
            MM(PS(kt), kmT[h, kt * 128:(kt + 1) * 128], q, True, True)
        for kt in range(2):
            pbuf = pt[(2 * i + kt) % 4].all()
            ACT(pbuf, PS(kt), AF.Exp, scale=SCALE)
            MM(PS(ob), vm[kt, h * 128:(h + 1) * 128], pbuf, kt == 0, kt == 1)
            MM(PS(lb), ones.all(), pbuf, kt == 0, kt == 1)

    def mem_back(i):
        h, qt = mem_units[i]
        ob, lb = (5, 6)[i % 2], (3, 4)[i % 2]
        RECIP(rl.all(), PS(lb))
        TT(memoT[h, qt * 512:(qt + 1) * 512], PS(ob), rl.all(), ALU.mult)

    def u_front(gi):
        j, t = gi // 2, gi % 2
        src_ps = PS(UB[gi % 2])
        for c in range(16):
            MM(src_ps, wzu[c, j * 128:(j + 1) * 128], hT[c, t * 512:(t + 1) * 512], c == 0, c == 15)
        t1 = gt[gi % 2][0].all()
        ACT(t1, src_ps, AF.Square)
        TS(t1, t1, C1, ALU.mult, 1.0, ALU.add)
        TT(t1, t1, src_ps, ALU.mult)

    def u_back(gi):
        j, t = gi // 2, gi % 2
        src_ps = PS(UB[gi % 2])
        t1, t2 = gt[gi % 2][0].all(), gt[gi % 2][1].all()
        ACT(t2, t1, AF.Sigmoid, scale=C2)
        TT(uT[j, t * 512:(t + 1) * 512], t2, src_ps, ALU.mult)

    for i in range(9):
        if i < 8:
            mem_front(i)
            u_front(i)
        if i >= 1:
            mem_back(i - 1)
        if i < 8:
            u_back(i)
    gi = 8
    dump("memoT", memoT.all())
    vg = [B(60, [512], F32), B(68, [512], F32)]
    vsq = B(62, [512], F32)
    vss = B(64, [1], F32)
    vsd = B(64.25, [1], F32)
    vrs = B(64.5, [1], F32)
    vn = [B(65, [512], BF16), B(70, [512], BF16)]
    gtmp = B(66, [512], F32)

    def p4_mm(tt):
        pb = 2 + (tt % 2)
        for c in range(16):
            MM(PS(pb), hT[c, tt * 128:(tt + 1) * 128], wzv[c], c == 0, c == 15)

    def p4_h1(tt, gi):
        src_ps = PS(2 + (tt % 2))
        t1, t2 = gt[gi % 2][0].all(), gt[gi % 2][1].all()
        out = vg[tt % 2].all()
        ACT(t1, src_ps, AF.Square); yield
        TS(t1, t1, C1, ALU.mult, 1.0, ALU.add); yield
        TT(t1, t1, src_ps, ALU.mult); yield
        ACT(t2, t1, AF.Sigmoid, scale=C2); yield
        TT(out, t2, src_ps, ALU.mult); yield

    def p4_h2(tt):
        g_, n_ = vg[tt % 2].all(), vn[tt % 2]
        ACT(vsq.all(), g_, AF.Square); yield
        RSUM(vss.all(), vsq.all()); yield
        ACT(vsd.all(), vss.all(), AF.Sqrt, scale=1.0 / 512.0, bias=eps_col); yield
        RECIP(vrs.all(), vsd.all()); yield
        STT(n_.all(), g_, vrs.all(), sgu.all(), ALU.mult, ALU.mult); yield
        pm = 4 + (tt % 2)
        for g in range(4):
            MM(Ref(banks[pm][:, g * 128:(g + 1) * 128], PS(pm).gran), n_[g * 128:(g + 1) * 128], wsT[g * 128:(g + 1) * 128], True, True)
        TT(gtmp.all(), PS(pm), bsp.all(), ALU.add); yield
        TT(gmT[:, tt * 128:(tt + 1) * 128], gtmp.all().v(lambda ap: ap.rearrange("p (g i) -> p g i", i=128)),
           uT[:, tt * 128:(tt + 1) * 128], ALU.mult); yield

    def interleave(gens):
        gens = [g for g in gens if g is not None]
        while gens:
            for g in list(gens):
                try:
                    next(g)
                except StopIteration:
                    gens.remove(g)

    p4_mm(0)
    for k in range(9):
        if k + 1 < 8:
            p4_mm(k + 1)
        interleave([p4_h2(k - 1) if k >= 1 else None, p4_h1(k, gi + k) if k < 8 else None])
    dump("gmT", gmT.all())

    gs = [[B(44 + 6 * i + 2 * k, [512], F32) for k in range(3)] for i in range(2)]
    acc = [B(56 + 2 * i, [512], F32) for i in range(2)]
    tmpm = [B(60 + 2 * i, [512], F32) for i in range(2)]
    rslot = [0]

    def ring_load(src_ap):
        i = (2, 1, 0)[rslot[0] % 3]
        rslot[0] += 1
        r = ring[i]
        pool_dma(r.all().v(lambda ap: ap.rearrange("p (a b) c -> p a (b c)", b=4)), src_ap, f"ring{i}")
        return r

    ysrc = [(attnT, 8, 48), (gmT, 4, 56), (memoT, 4, 60)]
    it = 0
    for c in range(16):
        w = ring_load(wp5_d[c])
        wv = lambda k, w=w: w.all().v(lambda ap: ap.rearrange("p a (b c) -> p (a b) c", c=128)[:, k, :])
        for t in range(2):
            tsl = slice(t * 512, (t + 1) * 512)
            i2 = it % 2
            for br in range(3):
                gb = (it * 3 + br) % 2
                for k in range(16):
                    MM(PS(gb), wv(br * 16 + k), hT[k, tsl], k == 0, k == 15)
                ACT(gs[i2][br].all(), PS(gb), AF.Sigmoid)
                src, nk, base = ysrc[br]
                yb = 2 + (it * 3 + br) % 2
                for k in range(nk):
                    MM(PS(yb), wv(base + k), src[k, tsl], k == 0, k == nk - 1)
                if br == 0:
                    TT(acc[i2].all(), gs[i2][0].all(), PS(yb), ALU.mult)
                elif br == 1:
                    TT(tmpm[i2].all(), gs[i2][1].all(), PS(yb), ALU.mult)
                    TT(acc[i2].all(), acc[i2].all(), tmpm[i2].all(), ALU.add)
                else:
                    TT(tmpm[i2].all(), gs[i2][2].all(), PS(yb), ALU.mult)
                    TT(mergedT[c, tsl], acc[i2].all(), tmpm[i2].all(), ALU.add)
            it += 1
    dump("mergedT", mergedT.all())

    for cg in range(4):
        for t in range(2):
            sp_dma(xres[4 * cg:4 * cg + 4, t * 512:(t + 1) * 512], xT_own[t][:, 4 * cg:4 * cg + 4, :], f"xres{cg}{t}")
    sq2 = [B(108 + i, [512], BF16) for i in range(4)]
    r2a = [B(124, [512], F32), B(126, [512], F32)]
    r2b = [B(128, [512], F32), B(130, [512], F32)]
    u6 = 0
    pend = []
    for cg in range(4):
        w = ring_load(wout_d[cg])
        for cc in range(4):
            c = cg * 4 + cc
            for t in range(2):
                tsl = slice(t * 512, (t + 1) * 512)
                pb = u6 % 6
                for k in range(16):
                    MM(PS(pb), w[k, cc * 128:(cc + 1) * 128], mergedT[k, tsl], k == 0, k == 15)
                TT(xres[c, tsl], xres[c, tsl], PS(pb), ALU.add)
                ACT(sq2[u6 % 4].all(), xres[c, tsl], AF.Square)
                pend.append((u6, c, t))
                if len(pend) > 2:
                    uu, c_, t_ = pend.pop(0)
                    MM(PS(6 + t_), ones.all(), sq2[uu % 4].all(), c_ == 0, c_ == 15)
                u6 += 1
    for uu, c_, t_ in pend:
        MM(PS(6 + t_), ones.all(), sq2[uu % 4].all(), c_ == 0, c_ == 15)
    dump("x1", xres.all())

    for t in range(2):
        tsl = slice(t * 512, (t + 1) * 512)
        ACT(r2a[t].all(), PS(6 + t), AF.Sqrt, scale=1.0 / D, bias=eps_col)
        RECIP(r2b[t].all(), r2a[t].all())
        for c in range(16):
            STT(h2T[c, tsl], xres[c, tsl], g_ffn(c), r2b[t].all(), ALU.mult, ALU.mult)

    rsq = [B(124 + 2 * i, [512], F32) for i in range(2)]
    rsq = [B(188 + 2 * i, [512], F32) for i in range(2)]
    ei = 0
    for G in range(4):
        for ub in range(4):
            w = ring_load(wup_d[G * 4 + ub])
            for jj in range(4):
                j = ub * 4 + jj
                for t in range(2):
                    tsl = slice(t * 512, (t + 1) * 512)
                    pb = ei % 2
                    for k in range(16):
                        MM(PS(pb), w[k, jj * 128:(jj + 1) * 128], h2T[k, tsl], k == 0, k == 15)
                    r = rsq[ei % 2].all()
                    ACT(r, PS(pb), AF.Relu)
                    TT(hid[j, tsl], r, r, ALU.mult)
                    ei += 1
        for cg in range(4):
            w = ring_load(wdn_d[G * 4 + cg])
            for cc in range(4):
                c = cg * 4 + cc
                for t in range(2):
                    tsl = slice(t * 512, (t + 1) * 512)
                    pb = 2 + (ei % 2)
                    ei += 1
                    for j in range(16):
                        MM(PS(pb), w[j, cc * 128:(cc + 1) * 128], hid[j, tsl], j == 0, j == 15)
                    TT(xres[c, tsl], xres[c, tsl], PS(pb), ALU.add)
            if G == 3:
                P.add("sp", (lambda cg: lambda e: e.dma_start(out=outT[:, 4 * cg:4 * cg + 4, :],
                                                               in_=xres[4 * cg:4 * cg + 4, :].ap))(cg),
                      reads=[xres[4 * cg:4 * cg + 4, :]], dma_slot=f"out{cg}")
    P.emit()
    st.close()
    return nc, P


def _rope_tables():
    rows = S // 64
    t_row = np.repeat(np.arange(rows), 64).astype(np.float64)
    t_col = np.tile(np.arange(64), rows).astype(np.float64)
    inv = 10000.0 ** (-np.arange(32, dtype=np.float64) / 32.0)
    ar = t_row[:, None] * inv
    ac = t_col[:, None] * inv
    ang = np.concatenate([ar, ar, ac, ac], axis=-1)
    cos = np.cos(ang)
    sin = np.sin(ang)
    sgn = np.ones(128, np.float64)
    sgn[0:32] = -1.0
    sgn[64:96] = -1.0
    return cos.astype(np.float32), (sin * sgn).astype(np.float32)


def _chunked(w, cols):
    K = w.shape[0] // 128
    return np.ascontiguousarray(w.reshape(K, 128, -1).transpose(1, 0, 2))


def _prep_inputs(inputs):
    f = lambda a: np.ascontiguousarray(np.asarray(a, dtype=np.float32))
    x = f(inputs["x"])[0]
    mem = f(inputs["mem"])[0]
    w_in = f(inputs["w_in"])[0]
    cos, sinS = _rope_tables()
    xT = x.reshape(16, 512, 16, 128).transpose(0, 3, 2, 1)
    xT_all = np.ascontiguousarray(xT)
    tabk = lambda t: np.ascontiguousarray(t.reshape(16, 4, 128, 128).transpose(0, 2, 1, 3))
    rep = lambda v: np.ascontiguousarray(np.broadcast_to(v[None, :], (128, v.shape[0])))
    gvec = np.concatenate([f(inputs[k])[0].reshape(16, 128).T for k in ("norm_mix", "mem_norm", "norm_ffn")], axis=1)
    grow = np.stack([rep(f(inputs[k])[0]) for k in ("q_norm", "k_norm", "mq_norm", "mk_norm")], axis=1)
    ws = f(inputs["w_spatial"])[0]
    wsT = np.ascontiguousarray(ws.transpose(2, 0, 1).reshape(128, 512))
    bsp = rep(f(inputs["b_spatial"])[0].reshape(512))
    w_ao = f(inputs["w_attn_o"])[0]
    w_go = f(inputs["w_gmlp_o"])[0]
    w_mo = f(inputs["w_mem_o"])[0]
    wp5 = np.empty((16, 128, 64, 128), np.float32)
    for c in range(16):
        cs = slice(c * 128, (c + 1) * 128)
        for br in range(3):
            wp5[c, :, br * 16:(br + 1) * 16, :] = _chunked(w_in[:, 3072 + br * 2048 + c * 128: 3072 + br * 2048 + (c + 1) * 128], 128)
        wp5[c, :, 48:56, :] = _chunked(w_ao[:, cs], 128)
        wp5[c, :, 56:60, :] = _chunked(w_go[:, cs], 128)
        wp5[c, :, 60:64, :] = _chunked(w_mo[:, cs], 128)
    w_out = f(inputs["w_out"])[0]
    wout = np.stack([_chunked(w_out[:, cg * 512:(cg + 1) * 512], 512) for cg in range(4)])
    w_up = f(inputs["w_ffn_up"])[0]
    wup = np.stack([_chunked(w_up[:, b * 512:(b + 1) * 512], 512) for b in range(16)])
    w_dn = f(inputs["w_ffn_down"])[0]
    wdn = np.stack([_chunked(w_dn[G * 2048:(G + 1) * 2048, cg * 512:(cg + 1) * 512], 512)
                    for G in range(4) for cg in range(4)])
    shared = dict(
        xT_all=xT_all,
        cos_k=tabk(cos), sin_k=tabk(sinS),
        ident=np.eye(128, dtype=np.float32),
        gvec=np.ascontiguousarray(gvec), grow=np.ascontiguousarray(grow),
        sgu=rep(f(inputs["sgu_norm"])[0]), bsp=bsp, wsT=wsT,
        memT=np.ascontiguousarray(mem.reshape(256, 16, 128).transpose(2, 1, 0)),
        wmkv=_chunked(f(inputs["w_mem_kv"])[0], 1024),
        wkv=_chunked(w_in[:, 1024:1536], 512),
        wq=_chunked(w_in[:, 0:1024], 1024),
        wqm=_chunked(w_in[:, 2560:3072], 512),
        wzu=_chunked(w_in[:, 1536:2048], 512),
        wzv=_chunked(w_in[:, 2048:2560], 512),
        wp5=wp5.reshape(16, 128, 4, 2048),
        wout=wout.reshape(4, 128, 4, 2048),
        wup=wup.reshape(16, 128, 4, 2048),
        wdn=wdn.reshape(16, 128, 4, 2048),
    )
    in_maps = []
    for i in range(NCORES):
        m = dict(shared)
        m["xT_own"] = np.ascontiguousarray(xT_all[2 * i:2 * i + 2])
        tq = lambda t: np.ascontiguousarray(t[i * TOK:(i + 1) * TOK].reshape(8, 128, 128).transpose(1, 0, 2))
        m["cos_q"] = tq(cos)
        m["sin_q"] = tq(sinS)
        in_maps.append(m)
    return in_maps


_NC_CACHE = {}


def kernel(**inputs):
    in_maps = _prep_inputs(inputs)
    if "nc" not in _NC_CACHE:
        _NC_CACHE["nc"] = build()[0]
    nc = _NC_CACHE["nc"]
    res = run_bass_kernel_spmd(nc, in_maps, core_ids=list(range(NCORES)))
    outs = []
    for i in range(NCORES):
        o = np.asarray(res.results[i]["outT"])
        outs.append(o.transpose(2, 1, 0).reshape(TOK, D))
    return np.concatenate(outs, axis=0)[None].astype(np.float32)
```

```python
import contextlib
import math
import numpy as np
import concourse.bass as bass
import concourse.mybir as mybir
from concourse.bass_utils import run_bass_kernel_spmd

F32 = mybir.dt.float32
BF16 = mybir.dt.bfloat16
AF = mybir.ActivationFunctionType
ALU = mybir.AluOpType
AX = mybir.AxisListType

NCORES = 8
S = 8192
D = 2048
TOK = S // NCORES
NCH = D // 128
EPS = 1e-6
SCALE = 128.0 ** -0.5
GRAN = 256
ARENA_KB = 196
ENGS = ("pe", "act", "dve", "pool", "sp")


class Ref:
    __slots__ = ("ap", "gran")

    def __init__(self, ap, gran):
        self.ap = ap
        self.gran = gran

    def v(self, f):
        return Ref(f(self.ap), self.gran)


class Buf:
    def __init__(self, arena, off, shape, dtype):
        isz = 4 if dtype == F32 else 2
        n = int(np.prod(shape))
        assert off % 4 == 0 and off + n * isz <= ARENA_KB * 1024, (off, shape)
        a = arena[:, off // 2: off // 2 + n * isz // 2]
        if dtype == F32:
            a = a.bitcast(F32)
        if len(shape) == 2:
            a = a.rearrange("p (a b) -> p a b", b=shape[1])
        elif len(shape) == 3:
            a = a.rearrange("p (a b c) -> p a b c", b=shape[1], c=shape[2])
        self.ap = a
        self.isz = isz
        self.idx = np.arange(n, dtype=np.int64).reshape(shape) * isz + off
        self.nd = len(shape)

    def __getitem__(self, key):
        if not isinstance(key, tuple):
            key = (key,)
        ap = self.ap[(slice(None),) + key]
        sub = np.asarray(self.idx[key]).ravel()
        lo = int(sub.min()) // GRAN
        g = np.unique(np.concatenate([sub // GRAN, (sub + self.isz - 1) // GRAN]))
        return Ref(ap, frozenset(g.tolist()))

    def all(self):
        return self[(slice(None),) * self.nd]


class Prog:
    def __init__(self, nc, same_engine_sync=("act", "dve", "pool")):
        self.nc = nc
        self.ins = []
        self.last_writer = {}
        self.readers = {}
        self.same_engine_sync = set(same_engine_sync)
        self.dma_slots = {}

    def add(self, eng, fn, reads=(), writes=(), dma_slot=None):
        idx = len(self.ins)
        deps = set()
        rg = set()
        for r in reads:
            rg |= r.gran
        wg = set()
        for r in writes:
            wg |= r.gran
        lw = self.last_writer
        rd = self.readers
        for g in rg:
            w = lw.get(g)
            if w is not None:
                deps.add(w)
        for g in wg:
            w = lw.get(g)
            if w is not None:
                deps.add(w)
            d = rd.get(g)
            if d:
                deps.update(d.values())
        deps.discard(idx)
        rec = dict(eng=eng, fn=fn, deps=deps, dma_slot=dma_slot, sig=None, needs_sig=False)
        if dma_slot is not None:
            n = self.dma_slots.get(dma_slot, 0) + 1
            self.dma_slots[dma_slot] = n
            rec["sig"] = (("dma", dma_slot), 16 * n)
        self.ins.append(rec)
        rkey = eng if dma_slot is None else ("dma", idx)
        for g in rg:
            rd.setdefault(g, {})[rkey] = idx
        for g in wg:
            lw[g] = idx
            rd[g] = {}
        return idx

    def emit(self):
        nc = self.nc
        ins = self.ins
        ses = self.same_engine_sync
        for r in ins:
            for d in r["deps"]:
                p = ins[d]
                if p["dma_slot"] is not None:
                    continue
                if p["eng"] != r["eng"] or r["dma_slot"] is not None or p["eng"] in ses:
                    p["needs_sig"] = True
        cnt = {e: 0 for e in ENGS}
        for r in ins:
            if r["dma_slot"] is None and r["needs_sig"]:
                cnt[r["eng"]] += 1
                r["sig"] = (("eng", r["eng"]), cnt[r["eng"]])
        stack = contextlib.ExitStack()
        sems = {}
        for e in ENGS:
            sems[("eng", e)] = stack.enter_context(nc.semaphore(f"s_{e}"))
        for s in self.dma_slots:
            sems[("dma", s)] = stack.enter_context(nc.semaphore(f"d_{s}"))
        per_eng = {e: [] for e in ENGS}
        for i, r in enumerate(ins):
            per_eng[r["eng"]].append(i)
        block = stack.enter_context(nc.Block())
        nwaits = [0]

        def run(eng_name, engine):
            waited = {}
            for i in per_eng[eng_name]:
                r = ins[i]
                need = {}
                for d in r["deps"]:
                    p = ins[d]
                    if p["sig"] is None:
                        continue
                    if (p["dma_slot"] is None and p["eng"] == eng_name and r["dma_slot"] is None
                            and eng_name not in ses):
                        continue
                    k, v = p["sig"]
                    if need.get(k, 0) < v:
                        need[k] = v
                for k, v in need.items():
                    if waited.get(k, 0) >= v:
                        continue
                    engine.wait_ge(sems[k], v)
                    nwaits[0] += 1
                    waited[k] = v
                inst = r["fn"](engine)
                if r["dma_slot"] is not None:
                    inst.then_inc(sems[("dma", r["dma_slot"])], 16)
                elif r["needs_sig"]:
                    inst.then_inc(sems[("eng", eng_name)], 1)
            if eng_name == "sp":
                for s, n in self.dma_slots.items():
                    k = ("dma", s)
                    if waited.get(k, 0) < 16 * n:
                        engine.wait_ge(sems[k], 16 * n)
                for e in ENGS:
                    if e != "sp" and cnt[e] > 0:
                        engine.wait_ge(sems[("eng", e)], cnt[e])

        @block.tensor
        def _(e):
            run("pe", e)

        @block.scalar
        def _(e):
            run("act", e)

        @block.vector
        def _(e):
            run("dve", e)

        @block.gpsimd
        def _(e):
            run("pool", e)

        @block.sync
        def _(e):
            run("sp", e)

        stack.close()
        self.stats = dict(n_ins=len(ins), sig=cnt, waits=nwaits[0],
                          per_eng={e: len(v) for e, v in per_eng.items()})


def build(stop_after=None, debug=()):
    nc = bass.Bass("TRN2", target_bir_lowering=False)
    st = contextlib.ExitStack()

    def din(name, shape, dt=F32):
        return nc.dram_tensor(name, list(shape), dt, kind="ExternalInput").ap()

    xT_all = din("xT_all", [16, 128, 16, 512])
    xT_own = din("xT_own", [2, 128, 16, 512])
    cos_k = din("cos_k", [16, 128, 4, 128])
    sin_k = din("sin_k", [16, 128, 4, 128])
    cos_q = din("cos_q", [128, 8, 128])
    sin_q = din("sin_q", [128, 8, 128])
    ident_d = din("ident", [128, 128])
    gvec_d = din("gvec", [128, 48])
    grow_d = din("grow", [128, 4, 128])
    sgu_d = din("sgu", [128, 512])
    bsp_d = din("bsp", [128, 512])
    wsT_d = din("wsT", [128, 512])
    memT_d = din("memT", [128, 16, 256])
    wmkv_d = din("wmkv", [128, 16, 1024])
    wkv_d = din("wkv", [128, 16, 512])
    wq_d = din("wq", [128, 16, 1024])
    wqm_d = din("wqm", [128, 16, 512])
    wzu_d = din("wzu", [128, 16, 512])
    wzv_d = din("wzv", [128, 16, 512])
    wp5_d = din("wp5", [16, 128, 4, 2048])
    wout_d = din("wout", [4, 128, 4, 2048])
    wup_d = din("wup", [16, 128, 4, 2048])
    wdn_d = din("wdn", [16, 128, 4, 2048])
    outT = nc.dram_tensor("outT", [128, 16, TOK], F32, kind="ExternalOutput").ap()
    dbg = {}
    dbg_shapes = dict(KT=([128, 2, 8192], BF16), V=([128, 64, 256], BF16), hT=([128, 16, 1024], BF16),
                      QT=([128, 8, 1024], BF16), attnT=([128, 8, 1024], BF16), memoT=([128, 4, 1024], BF16),
                      gmT=([128, 4, 1024], BF16), mergedT=([128, 16, 1024], BF16), kmT=([128, 4, 256], BF16),
                      vm=([128, 2, 512], BF16), x1=([128, 16, 1024], F32))
    for k in debug:
        shp, dt = dbg_shapes[k]
        dbg[k] = nc.dram_tensor("dbg_" + k, shp, dt, kind="ExternalOutput").ap()

    arena = st.enter_context(nc.sbuf_tensor("arena", [128, ARENA_KB * 512], BF16))
    banks = [st.enter_context(nc.psum_tensor(f"bank{i}", [128, 512], F32)) for i in range(8)]

    def PS(i, cols=512, dt=F32):
        b = banks[i]
        if dt == BF16:
            ap = b.bitcast(BF16)[:, 0:cols]
        else:
            ap = b[:, 0:cols]
        return Ref(ap, frozenset([("ps", i)]))

    KB = 1024
    P = Prog(nc)

    def B(off_kb, shape, dt):
        return Buf(arena, int(off_kb * KB), shape, dt)

    ident = B(0, [128], BF16)
    ones = B(0.25, [128], BF16)
    gvec = B(0.5, [48], F32)
    small = B(0.75, [64], F32)
    grow = B(1, [4, 128], F32)
    sgu = B(3, [512], F32)
    bsp = B(5, [512], F32)
    wsT = B(7, [512], BF16)
    kmT = B(8, [4, 256], BF16)
    vm = B(10, [2, 512], BF16)
    hT = B(12, [16, 1024], BF16)
    xres = B(12, [16, 1024], F32)
    KT = B(44, [2, 8192], BF16)
    V = B(76, [64, 256], BF16)
    mergedT = B(76, [16, 1024], BF16)
    h2T = B(76, [16, 1024], BF16)
    attnT = B(108, [8, 1024], BF16)
    memoT = B(124, [4, 1024], BF16)
    gmT = B(132, [4, 1024], BF16)
    hid = B(108, [16, 1024], BF16)
    ring = [B(140 + 16 * i, [16, 512], BF16) for i in range(3)]

    def sp_dma(out_ref, in_ap, slot):
        P.add("sp", lambda e: e.dma_start(out=out_ref.ap, in_=in_ap), writes=[out_ref], dma_slot=slot)

    def pool_dma(out_ref, in_ap, slot):
        P.add("pool", lambda e: e.dma_start(out=out_ref.ap, in_=in_ap), writes=[out_ref], dma_slot=slot)

    def MM(out, lhsT, rhs, start, stop):
        P.add("pe", lambda e: e.matmul(out.ap, lhsT=lhsT.ap, rhs=rhs.ap, start=start, stop=stop),
              reads=[lhsT, rhs], writes=[out])

    def TR(out, in_):
        P.add("pe", lambda e: e.transpose(out=out.ap, in_=in_.ap, identity=ident.all().ap),
              reads=[in_, ident.all()], writes=[out])

    def ACT(out, in_, func, scale=None, bias=None, extra_reads=()):
        kw = {}
        if scale is not None:
            kw["scale"] = scale if not isinstance(scale, Ref) else scale.ap
        if bias is not None:
            kw["bias"] = bias if not isinstance(bias, Ref) else bias.ap
        rds = [in_] + [x for x in (scale, bias) if isinstance(x, Ref)] + list(extra_reads)
        P.add("act", lambda e: e.activation(out=out.ap, in_=in_.ap, func=func, **kw), reads=rds, writes=[out])

    def TT(out, in0, in1, op, eng="dve"):
        P.add(eng, lambda e: e.tensor_tensor(out=out.ap, in0=in0.ap, in1=in1.ap, op=op),
              reads=[in0, in1], writes=[out])

    def TS(out, in0, s1, op0, s2=None, op1=None, eng="dve"):
        rds = [in0] + [x for x in (s1, s2) if isinstance(x, Ref)]
        a1 = s1.ap if isinstance(s1, Ref) else s1
        a2 = s2.ap if isinstance(s2, Ref) else s2
        if op1 is None:
            P.add(eng, lambda e: e.tensor_scalar(out=out.ap, in0=in0.ap, scalar1=a1, scalar2=None, op0=op0),
                  reads=rds, writes=[out])
        else:
            P.add(eng, lambda e: e.tensor_scalar(out=out.ap, in0=in0.ap, scalar1=a1, scalar2=a2, op0=op0, op1=op1),
                  reads=rds, writes=[out])

    def STT(out, in0, scalar, in1, op0, op1):
        rds = [in0, in1] + ([scalar] if isinstance(scalar, Ref) else [])
        sc = scalar.ap if isinstance(scalar, Ref) else scalar
        P.add("dve", lambda e: e.scalar_tensor_tensor(out=out.ap, in0=in0.ap, scalar=sc, in1=in1.ap, op0=op0, op1=op1),
              reads=rds, writes=[out])

    def RECIP(out, in_):
        P.add("dve", lambda e: e.reciprocal(out=out.ap, in_=in_.ap), reads=[in_], writes=[out])

    def RSUM(out, in_):
        P.add("dve", lambda e: e.reduce_sum(out=out.ap, in_=in_.ap, axis=AX.X), reads=[in_], writes=[out])

    def COPY(out, in_, eng="dve"):
        if eng == "act":
            ACT(out, in_, AF.Copy)
        else:
            P.add(eng, lambda e: e.tensor_copy(out=out.ap, in_=in_.ap), reads=[in_], writes=[out])

    def dump(name, ref):
        if name in dbg:
            P.add("sp", lambda e: e.dma_start(out=dbg[name], in_=ref.ap), reads=[ref], dma_slot="dbg_" + name)

    def bc(ap, shape):
        return ap.broadcast_to(shape)

    def qk_pipeline(src, G, nh, gain, cos, sin, scr, out):
        if G >= 2:
            gs_, hs_ = [slice(0, G // 2), slice(G // 2, G)], [slice(0, nh)] * 2
            Gg, nhg = G // 2, nh
        else:
            gs_, hs_ = [slice(0, 1)] * 2, [slice(0, nh // 2), slice(nh // 2, nh)]
            Gg, nhg = G, nh // 2
        g4 = lambda ap: ap.rearrange("p g (h d) -> p g h d", d=128)
        s3 = lambda ap: ap.rearrange("p (g h) -> p g h", h=nh)

        def big(ref, i):
            return ref.v(lambda ap: g4(ap)[:, gs_[i], hs_[i], :])

        def sm(ref, i):
            return ref.v(lambda ap: s3(ap)[:, gs_[i], hs_[i]])

        def tab(ref, i):
            return ref.v(lambda ap: ap[:, gs_[i], :])

        sq, qn, a_, b_ = scr["sq"].all(), scr["qn"].all(), scr["a"].all(), scr["b"].all()
        ss, sd, rs = scr["ss"].all(), scr["sd"].all(), scr["rs"].all()
        R2 = range(2)
        for i in R2:
            ACT(big(sq, i), big(src, i), AF.Square)
        for i in R2:
            RSUM(sm(ss, i), big(sq, i))
        for i in R2:
            ACT(sm(sd, i), sm(ss, i), AF.Sqrt, scale=1.0 / 128.0, bias=eps_col)
        for i in R2:
            RECIP(sm(rs, i), sm(sd, i))
        for i in R2:
            TT(big(qn, i), big(src, i), sm(rs, i).v(lambda ap: bc(ap.unsqueeze(3), [128, Gg, nhg, 128])), ALU.mult)
        dst = qn if cos is not None else out
        for i in R2:
            TT(big(dst, i), big(qn, i), gain.v(lambda ap: bc(ap.unsqueeze(1).unsqueeze(1), [128, Gg, nhg, 128])), ALU.mult)
        if cos is None:
            return
        for i in R2:
            TT(big(a_, i), big(qn, i), tab(cos, i).v(lambda ap: bc(ap.unsqueeze(2), [128, Gg, nhg, 128])), ALU.mult)
        r5 = lambda ap: ap.rearrange("p g h (f r i) -> p g h f r i", f=2, r=2, i=32)
        t5 = lambda ap: ap.rearrange("p g (f r i) -> p g f r i", f=2, r=2, i=32)
        for r in range(2):
            for i in R2:
                if Gg == 1:
                    TT(big(b_, i).v(lambda ap: r5(ap)[:, 0, :, :, r, :]), big(qn, i).v(lambda ap: r5(ap)[:, 0, :, :, 1 - r, :]),
                       tab(sin, i).v(lambda ap: bc(t5(ap)[:, :, :, r, :], [128, nhg, 2, 32])), ALU.mult)
                else:
                    for h in range(nhg):
                        TT(big(b_, i).v(lambda ap: r5(ap)[:, :, h, :, r, :]), big(qn, i).v(lambda ap: r5(ap)[:, :, h, :, 1 - r, :]),
                           tab(sin, i).v(lambda ap: t5(ap)[:, :, :, r, :]), ALU.mult)
        for i in R2:
            TT(big(out, i), big(a_, i), big(b_, i), ALU.add)

    eps_col = small[0:1]
    one_f = small[1:2]
    P.add("dve", lambda e: e.memset(eps_col.ap, EPS), writes=[eps_col])
    P.add("dve", lambda e: e.memset(one_f.ap, 1.0), writes=[one_f])
    w32 = small[2:3]
    P.add("dve", lambda e: e.memset(w32.ap, 1.0 / 32.0), writes=[w32])
    P.add("dve", lambda e: e.memset(ones.all().ap, 1.0), writes=[ones.all()])
    pool_dma(ident.all(), ident_d, "c_ident")
    sp_dma(gvec.all(), gvec_d, "c_gvec")
    sp_dma(grow.all(), grow_d, "c_grow")
    sp_dma(sgu.all(), sgu_d, "c_sgu")
    sp_dma(bsp.all(), bsp_d, "c_bsp")
    pool_dma(wsT.all(), wsT_d, "c_wsT")
    g_mix = lambda c: gvec[c:c + 1]
    g_mem = lambda c: gvec[16 + c:17 + c]
    g_ffn = lambda c: gvec[32 + c:33 + c]
    g_q, g_k, g_mq, g_mk = grow[0], grow[1], grow[2], grow[3]

    wmkv = B(12, [16, 1024], BF16)
    memT = B(44, [16, 256], F32)
    memn = B(60, [16, 256], BF16)
    msq = B(68, [16, 256], BF16)
    mrs = B(108, [256], F32)
    mrs2 = B(109, [256], F32)
    pool_dma(wmkv.all(), wmkv_d, "w_mkv")
    sp_dma(memT.all(), memT_d, "x_mem")
    for c4 in range(4):
        ACT(msq[4 * c4:4 * c4 + 4], memT[4 * c4:4 * c4 + 4], AF.Square)
    for c in range(16):
        MM(PS(0, 256), ones.all(), msq[c], c == 0, c == 15)
    ACT(mrs.all(), PS(0, 256), AF.Sqrt, scale=1.0 / D, bias=eps_col)
    RECIP(mrs2.all(), mrs.all())
    for c in range(16):
        STT(memn[c], memT[c], g_mem(c), mrs2.all(), ALU.mult, ALU.mult)
    mscr = dict(sq=B(110, [1, 512], F32), qn=B(112, [1, 512], F32), a=B(110, [1, 512], F32), b=B(110, [1, 512], F32),
                ss=B(114, [4], F32), sd=B(114.25, [4], F32), rs=B(114.5, [4], F32))
    kmn = B(115, [1, 512], BF16)
    for mt in range(2):
        for half in range(2):
            for c in range(16):
                MM(PS(1 + half), memn[c, mt * 128:(mt + 1) * 128], wmkv[c, half * 512:(half + 1) * 512], c == 0, c == 15)
        qk_pipeline(PS(1).v(lambda ap: ap.unsqueeze(1)), 1, 4, g_mk, None, None, mscr, kmn.all())
        for h in range(4):
            TR(Ref(PS(3, 1024, BF16).ap[:, h * 128:(h + 1) * 128], PS(3).gran), kmn[0, h * 128:(h + 1) * 128])
        COPY(kmT[:, mt * 128:(mt + 1) * 128],
             PS(3, 512, BF16).v(lambda ap: ap.rearrange("p (h m) -> p h m", m=128)))
        COPY(vm[mt], PS(2), eng="act")
    dump("kmT", kmT.all())
    dump("vm", vm.all())

    wkv_f = B(76, [16, 512], F32)
    wkv_g = B(108, [16, 512], BF16)
    sp_dma(wkv_f.all(), wkv_d, "w_kv")
    for c in range(16):
        TS(wkv_g[c], wkv_f[c], g_mix(c), ALU.mult)

    xs = [B(140, [16, 512], BF16), B(156, [16, 512], BF16)]
    sqs = B(172, [16, 512], BF16)
    tabs = [(B(124, [4, 128], F32), B(126, [4, 128], F32)), (B(128, [4, 128], F32), B(130, [4, 128], F32))]
    kv_sb = [B(12, [4, 512], F32), B(20, [4, 512], F32)]
    kscr = dict(sq=B(28, [4, 256], F32), qn=B(32, [4, 256], F32), a=B(36, [4, 256], F32), b=B(40, [4, 256], F32),
                ss=B(138, [8], F32), sd=B(138.125, [8], F32), rs=B(138.25, [8], F32))
    krope = [B(132, [4, 256], BF16), B(134, [4, 256], BF16)]
    sd_bc = B(136, [512], F32)
    rtok = [B(138.5, [4], F32), B(138.75, [4], F32)]
    KVB = [2, 3, 6, 7]
    NT1 = 16

    def p1_load(T):
        b = T % 2
        pool_dma(xs[b].all(), xT_all[T], f"xs{b}")

    def p1_tabs(T):
        b = T % 2
        sp_dma(tabs[b][0].all(), cos_k[T], f"tc{b}")
        sp_dma(tabs[b][1].all(), sin_k[T], f"ts{b}")

    def p1_squares(T):
        b = T % 2
        for c4 in range(4):
            if c4 < 3:
                ACT(sqs[4 * c4:4 * c4 + 4], xs[b][4 * c4:4 * c4 + 4], AF.Square)
            else:
                TT(sqs[4 * c4:4 * c4 + 4], xs[b][4 * c4:4 * c4 + 4], xs[b][4 * c4:4 * c4 + 4], ALU.mult)

    sdt = [B(139, [4], F32), B(139.25, [4], F32)]

    def p1_stats(T):
        b = T % 2
        for c in range(16):
            j = c % 4
            P.add("pe", (lambda c, j: lambda e: e.matmul(banks[0][32 * j:32 * j + 32, :], lhsT=ones.all().ap[:, 0:32],
                                                          rhs=sqs[c].ap, start=(c < 4), stop=(c >= 12),
                                                          tile_position=(0, 32 * j)))(c, j),
                  reads=[ones.all(), sqs[c]], writes=[PS(0)])
        ACT(sd_bc.all(), PS(0), AF.Copy)
        for sub in range(4):
            MM(Ref(banks[1][:, sub:sub + 1], PS(1).gran),
               sd_bc.all().v(lambda ap: ap[:, sub * 128:(sub + 1) * 128]), w32, True, True)
        ACT(sdt[b].all(), Ref(banks[1][:, 0:4], PS(1).gran), AF.Sqrt, scale=1.0 / D, bias=eps_col)
        RECIP(rtok[b].all(), sdt[b].all())

    def p1_B(T):
        b = T % 2
        for sub in range(4):
            pb = KVB[sub]
            for c in range(16):
                MM(PS(pb), xs[b][c, sub * 128:(sub + 1) * 128], wkv_g[c], c == 0, c == 15)
            ACT(kv_sb[b][sub], PS(pb), AF.Copy, scale=rtok[b][sub:sub + 1])

    def p1_C(T):
        b = T % 2
        COPY(V[T * 4:T * 4 + 4], kv_sb[b][:, 256:512], eng="act")
        qk_pipeline(kv_sb[b][:, 0:256], 4, 2, g_k, tabs[b][0].all(), tabs[b][1].all(), kscr, krope[b].all())

    def p1_D(T):
        b = T % 2
        pt_b = 4 + b
        for sub in range(4):
            for h in range(2):
                j = sub * 2 + h
                TR(Ref(PS(pt_b, 1024, BF16).ap[:, j * 128:(j + 1) * 128], PS(pt_b).gran), krope[b][sub, h * 128:(h + 1) * 128])
        COPY(KT[:, T * 512:(T + 1) * 512].v(lambda ap: ap.rearrange("p h (s j) -> p h s j", j=128)),
             PS(pt_b, 1024, BF16).v(lambda ap: ap.rearrange("p (s h j) -> p h s j", h=2, j=128)), eng="dve")

    p1_load(0)
    p1_tabs(0)
    p1_squares(0)
    for T in range(NT1 + 1):
        if T + 1 < NT1:
            p1_load(T + 1)
        if T < NT1:
            p1_stats(T)
        if T >= 1:
            p1_C(T - 1)
        if T + 1 < NT1:
            p1_tabs(T + 1)
            p1_squares(T + 1)
        if T < NT1:
            p1_B(T)
        if T >= 1:
            p1_D(T - 1)
    dump("KT", KT.all())
    dump("V", V.all())

    wq = B(108, [16, 1024], BF16)
    pool_dma(wq.all(), wq_d, "w_q")
    xo = [B(140, [16, 256], F32), B(156, [16, 256], F32)]
    sqo = [B(172, [16, 256], BF16), B(180, [16, 256], BF16)]
    rbc = [B(188, [256], F32), B(189, [256], F32)]
    rbc2 = [B(190, [256], F32), B(191, [256], F32)]
    for hh in range(4):
        t, o = hh // 2, (hh % 2) * 256
        b = hh % 2
        sp_dma(xo[b].all(), xT_own[t][:, :, o:o + 256], f"xo{b}")
        for c4 in range(4):
            ACT(sqo[b][4 * c4:4 * c4 + 4], xo[b][4 * c4:4 * c4 + 4], AF.Square)
        for c in range(16):
            MM(PS(b, 256), ones.all(), sqo[b][c], c == 0, c == 15)
        ACT(rbc[b].all(), PS(b, 256), AF.Sqrt, scale=1.0 / D, bias=eps_col)
        RECIP(rbc2[b].all(), rbc[b].all())
        for c in range(16):
            STT(hT[c, t * 512 + o:t * 512 + o + 256], xo[b][c], g_mix(c), rbc2[b].all(), ALU.mult, ALU.mult)
    dump("hT", hT.all())

    QT = B(172, [8, 1024], BF16)
    tq = (B(188, [8, 128], F32), B(192, [8, 128], F32))
    sp_dma(tq[0].all(), cos_q, "tqc")
    sp_dma(tq[1].all(), sin_q, "tqs")
    qscr = dict(sq=B(140, [1, 1024], F32), qn=B(144, [1, 1024], F32), a=B(148, [1, 1024], F32), b=B(152, [1, 1024], F32),
                ss=B(168, [8], F32), sd=B(168.25, [8], F32), rs=B(168.5, [8], F32))
    qrope = [B(156, [1, 1024], BF16), B(158, [1, 1024], BF16)]
    qsb = [B(160, [1, 1024], F32), B(164, [1, 1024], F32)]

    def p2_mm(tt):
        b = tt % 2
        for half in range(2):
            pb = 2 * b + half
            for c in range(16):
                MM(PS(pb), hT[c, tt * 128:(tt + 1) * 128], wq[c, half * 512:(half + 1) * 512], c == 0, c == 15)

    gsw = B(169, [128], F32)
    for blk, src_blk in ((0, 1), (1, 0), (2, 3), (3, 2)):
        COPY(gsw[blk * 32:(blk + 1) * 32], g_q.v(lambda ap: ap[:, src_blk * 32:(src_blk + 1) * 32]), eng="act")
    TT(tq[0].all(), tq[0].all(), g_q.v(lambda ap: bc(ap.unsqueeze(1), [128, 8, 128])), ALU.mult)
    TT(tq[1].all(), tq[1].all(), gsw.all().v(lambda ap: bc(ap.unsqueeze(1), [128, 8, 128])), ALU.mult)

    def p2_pipe(tt):
        b = tt % 2
        sq, qn, a_, b_ = qscr["sq"], qscr["qn"], qscr["a"], qscr["b"]
        ss, sd, rs = qscr["ss"], qscr["sd"], qscr["rs"]
        R2 = range(2)
        hsl = lambda i: slice(i * 512, (i + 1) * 512)
        h4 = lambda ap: ap.rearrange("p (h d) -> p h d", d=128)
        r5 = lambda ap: ap.rearrange("p (h f r i) -> p h f r i", f=2, r=2, i=32)
        for i in R2:
            ACT(sq[0, hsl(i)], PS(2 * b + i), AF.Square)
        for i in R2:
            RSUM(ss[4 * i:4 * i + 4], sq[0, hsl(i)].v(h4))
        for i in R2:
            ACT(sd[4 * i:4 * i + 4], ss[4 * i:4 * i + 4], AF.Sqrt, scale=1.0 / 128.0, bias=eps_col)
        for i in R2:
            RECIP(rs[4 * i:4 * i + 4], sd[4 * i:4 * i + 4])
        for h in range(4):
            for i in R2:
                j = 4 * i + h
                ACT(qn[0, j * 128:(j + 1) * 128], PS(2 * b + i).v(lambda ap: ap[:, h * 128:(h + 1) * 128]), AF.Copy,
                    scale=rs[j:j + 1])
        for i in R2:
            TT(a_[0, hsl(i)].v(h4), qn[0, hsl(i)].v(h4), tq[0][tt].v(lambda ap: bc(ap.unsqueeze(1), [128, 4, 128])), ALU.mult)
        for r in range(2):
            for i in R2:
                TT(b_[0, hsl(i)].v(lambda ap: r5(ap)[:, :, :, r, :]), qn[0, hsl(i)].v(lambda ap: r5(ap)[:, :, :, 1 - r, :]),
                   tq[1][tt].v(lambda ap: bc(ap.rearrange("p (f r i) -> p f r i", f=2, r=2, i=32)[:, :, r, :].unsqueeze(1), [128, 4, 2, 32])),
                   ALU.mult)
        for i in R2:
            TT(qrope[b][0, hsl(i)], a_[0, hsl(i)], b_[0, hsl(i)], ALU.add)

    def p2_tr(tt):
        b = tt % 2
        pb = 4 + b
        for h in range(8):
            TR(Ref(PS(pb, 1024, BF16).ap[:, h * 128:(h + 1) * 128], PS(pb).gran), qrope[b][0, h * 128:(h + 1) * 128])
        COPY(QT[:, tt * 128:(tt + 1) * 128], PS(pb, 1024, BF16).v(lambda ap: ap.rearrange("p (h j) -> p h j", j=128)),
             eng="dve")

    p2_mm(0)
    for tt in range(8):
        if tt + 1 < 8:
            p2_mm(tt + 1)
        p2_pipe(tt)
        p2_tr(tt)
    dump("QT", QT.all())

    pt = [B(132 + i, [512], BF16) for i in range(4)]
    rl = B(136, [512], F32)

    accL = [B(124, [512], F32), B(126, [512], F32)]
    ones_f = B(138, [128], F32)

    def attention(n_kt, qT_of, kT_of, v_of, out_of, units, sbank, obank, lbank, half_l=False):
        ahead = len(sbank) - 1
        deferred = []

        def S(u, q, kt):
            MM(PS(sbank[kt % len(sbank)]), kT_of(u, kt), q, True, True)

        for ui, u in enumerate(units):
            ob = obank[ui % len(obank)]
            lb = lbank[ui % len(lbank)]
            acc = accL[ui % 2].all()
            q = qT_of(u)
            if ui == 0:
                for k0 in range(min(ahead, n_kt)):
                    S(u, q, k0)
            defer_now = n_kt >= 32
            for kt in range(n_kt):
                pbuf = pt[kt % len(pt)].all()
                ACT(pbuf, PS(sbank[kt % len(sbank)]), AF.Exp, scale=SCALE)
                if kt + ahead < n_kt:
                    S(u, q, kt + ahead)
                MM(PS(ob), v_of(u, kt), pbuf, kt == 0, kt == n_kt - 1)
                if half_l and kt % 2 == 1:
                    if kt == 1:
                        COPY(acc, pbuf, eng="dve")
                    else:
                        TT(acc, acc, pbuf, ALU.add)
                else:
                    MM(PS(lb), ones.all(), pbuf, kt == 0, (kt == n_kt - 1) and not half_l)
                while deferred and deferred[0][0] <= kt:
                    deferred.pop(0)[1]()
            if ui + 1 < len(units):
                un = units[ui + 1]
                qn_ = qT_of(un)
                for k0 in range(min(ahead, n_kt)):
                    S(un, qn_, k0)
            if half_l:
                MM(PS(lb), ones_f.all(), acc, False, True)

            def mk_epilogue(ob=ob, lb=lb, u=u):
                fns = []
                for pc in range(4):
                    csl = slice(pc * 128, (pc + 1) * 128)
                    fns.append(lambda csl=csl: RECIP(rl[csl], PS(lb).v(lambda ap: ap[:, csl])))
                fns.append(lambda: TT(out_of(u), PS(ob), rl.all(), ALU.mult))
                return fns

            fns = mk_epilogue()
            if defer_now and ui + 1 < len(units):
                deferred = [(6 + 4 * i, f) for i, f in enumerate(fns)]
            else:
                for f in fns:
                    f()
        for _, f in deferred:
            f()

    P.add("dve", lambda e: e.memset(ones_f.all().ap, 1.0), writes=[ones_f.all()])
    units = [(h, qt) for qt in range(2) for h in range(8)]
    attention(64,
              lambda u: QT[u[0], u[1] * 512:(u[1] + 1) * 512],
              lambda u, kt: KT[u[0] // 4, kt * 128:(kt + 1) * 128],
              lambda u, kt: V[kt, (u[0] // 4) * 128:(u[0] // 4 + 1) * 128],
              lambda u: attnT[u[0], u[1] * 512:(u[1] + 1) * 512],
              units, [0, 1, 2, 7], [3, 4], [5, 6], half_l=True)
    dump("attnT", attnT.all())

    wqm = B(140, [16, 512], BF16)
    qmT = B(188, [4, 1024], BF16)
    wzu = B(156, [16, 512], BF16)
    pool_dma(wqm.all(), wqm_d, "w_qm")
    pool_dma(wzu.all(), wzu_d, "w_zu")
    m3 = dict(sq=B(44, [1, 512], F32), qn=B(46, [1, 512], F32), a=B(44, [1, 512], F32), b=B(44, [1, 512], F32),
              ss=B(48, [4], F32), sd=B(48.25, [4], F32), rs=B(48.5, [4], F32))
    qmn = [B(49, [1, 512], BF16), B(50, [1, 512], BF16)]

    def p3_mm(tt):
        pb = 6 + tt % 2
        for c in range(16):
            MM(PS(pb), hT[c, tt * 128:(tt + 1) * 128], wqm[c], c == 0, c == 15)

    def p3_post(tt):
        pb = 6 + tt % 2
        b = tt % 2
        qk_pipeline(PS(pb).v(lambda ap: ap.unsqueeze(1)), 1, 4, g_mq, None, None, m3, qmn[b].all())
        pb2 = 3 + b
        for h in range(4):
            TR(Ref(PS(pb2, 1024, BF16).ap[:, h * 128:(h + 1) * 128], PS(pb2).gran), qmn[b][0, h * 128:(h + 1) * 128])
        COPY(qmT[:, tt * 128:(tt + 1) * 128], PS(pb2, 512, BF16).v(lambda ap: ap.rearrange("p (h j) -> p h j", j=128)),
             eng="dve")

    for tt in range(9):
        if tt < 8:
            p3_mm(tt)
        if tt >= 1:
            p3_post(tt - 1)
    units = [(h, qt) for qt in range(2) for h in range(4)]
    attention(2,
              lambda u: qmT[u[0], u[1] * 512:(u[1] + 1) * 512],
              lambda u, kt: kmT[u[0], kt * 128:(kt + 1) * 128],
              lambda u, kt: vm[kt, u[0] * 128:(u[0] + 1) * 128],
              lambda u: memoT[u[0], u[1] * 512:(u[1] + 1) * 512],
              units, [0, 1, 2], [5, 6], [3, 4])
    dump("memoT", memoT.all())

    wzv = B(140, [16, 512], BF16)
    uT = B(76, [4, 1024], F32)
    pool_dma(wzv.all(), wzv_d, "w_zv")
    C1 = 0.044715
    C2 = 2.0 * math.sqrt(2.0 / math.pi)
    gt = [(B(52 + 4 * i, [512], F32), B(54 + 4 * i, [512], F32)) for i in range(2)]

    def gelu(out, src_ps, i):
        t1, t2 = gt[i % 2][0].all(), gt[i % 2][1].all()
        ACT(t1, src_ps, AF.Square)
        TS(t1, t1, C1, ALU.mult, 1.0, ALU.add)
        TT(t1, t1, src_ps, ALU.mult)
        ACT(t2, t1, AF.Sigmoid, scale=C2)
        TT(out, t2, src_ps, ALU.mult)

    gi = 0
    for j in range(4):
        for t in range(2):
            pb = gi % 2
            for c in range(16):
                MM(PS(pb), wzu[c, j * 128:(j + 1) * 128], hT[c, t * 512:(t + 1) * 512], c == 0, c == 15)
            gelu(uT[j, t * 512:(t + 1) * 512], PS(pb), gi)
            gi += 1
    vg = [B(60, [512], F32), B(68, [512], F32)]
    vsq = B(62, [512], F32)
    vss = B(64, [1], F32)
    vsd = B(64.25, [1], F32)
    vrs = B(64.5, [1], F32)
    vn = [B(65, [512], BF16), B(70, [512], BF16)]
    gtmp = B(66, [512], F32)

    def p4_mm(tt):
        pb = 2 + (tt % 2)
        for c in range(16):
            MM(PS(pb), hT[c, tt * 128:(tt + 1) * 128], wzv[c], c == 0, c == 15)

    def p4_h1(tt, gi):
        src_ps = PS(2 + (tt % 2))
        t1, t2 = gt[gi % 2][0].all(), gt[gi % 2][1].all()
        out = vg[tt % 2].all()
        ACT(t1, src_ps, AF.Square); yield
        TS(t1, t1, C1, ALU.mult, 1.0, ALU.add); yield
        TT(t1, t1, src_ps, ALU.mult); yield
        ACT(t2, t1, AF.Sigmoid, scale=C2); yield
        TT(out, t2, src_ps, ALU.mult); yield

    def p4_h2(tt):
        g_, n_ = vg[tt % 2].all(), vn[tt % 2]
        ACT(vsq.all(), g_, AF.Square); yield
        RSUM(vss.all(), vsq.all()); yield
        ACT(vsd.all(), vss.all(), AF.Sqrt, scale=1.0 / 512.0, bias=eps_col); yield
        RECIP(vrs.all(), vsd.all()); yield
        STT(n_.all(), g_, vrs.all(), sgu.all(), ALU.mult, ALU.mult); yield
        pm = 4 + (tt % 2)
        for g in range(4):
            MM(Ref(banks[pm][:, g * 128:(g + 1) * 128], PS(pm).gran), n_[g * 128:(g + 1) * 128], wsT[g * 128:(g + 1) * 128], True, True)
        TT(gtmp.all(), PS(pm), bsp.all(), ALU.add); yield
        TT(gmT[:, tt * 128:(tt + 1) * 128], gtmp.all().v(lambda ap: ap.rearrange("p (g i) -> p g i", i=128)),
           uT[:, tt * 128:(tt + 1) * 128], ALU.mult); yield

    def interleave(gens):
        gens = [g for g in gens if g is not None]
        while gens:
            for g in list(gens):
                try:
                    next(g)
                except StopIteration:
                    gens.remove(g)

    p4_mm(0)
    for k in range(9):
        if k + 1 < 8:
            p4_mm(k + 1)
        interleave([p4_h2(k - 1) if k >= 1 else None, p4_h1(k, gi + k) if k < 8 else None])
    dump("gmT", gmT.all())

    gs = [[B(44 + 6 * i + 2 * k, [512], F32) for k in range(3)] for i in range(2)]
    acc = [B(56 + 2 * i, [512], F32) for i in range(2)]
    tmpm = [B(60 + 2 * i, [512], F32) for i in range(2)]
    rslot = [0]

    def ring_load(src_ap):
        i = (2, 1, 0)[rslot[0] % 3]
        rslot[0] += 1
        r = ring[i]
        pool_dma(r.all().v(lambda ap: ap.rearrange("p (a b) c -> p a (b c)", b=4)), src_ap, f"ring{i}")
        return r

    ysrc = [(attnT, 8, 48), (gmT, 4, 56), (memoT, 4, 60)]
    it = 0
    for c in range(16):
        w = ring_load(wp5_d[c])
        wv = lambda k, w=w: w.all().v(lambda ap: ap.rearrange("p a (b c) -> p (a b) c", c=128)[:, k, :])
        for t in range(2):
            tsl = slice(t * 512, (t + 1) * 512)
            i2 = it % 2
            for br in range(3):
                gb = (it * 3 + br) % 2
                for k in range(16):
                    MM(PS(gb), wv(br * 16 + k), hT[k, tsl], k == 0, k == 15)
                ACT(gs[i2][br].all(), PS(gb), AF.Sigmoid)
                src, nk, base = ysrc[br]
                yb = 2 + (it * 3 + br) % 2
                for k in range(nk):
                    MM(PS(yb), wv(base + k), src[k, tsl], k == 0, k == nk - 1)
                if br == 0:
                    TT(acc[i2].all(), gs[i2][0].all(), PS(yb), ALU.mult)
                elif br == 1:
                    TT(tmpm[i2].all(), gs[i2][1].all(), PS(yb), ALU.mult)
                    TT(acc[i2].all(), acc[i2].all(), tmpm[i2].all(), ALU.add)
                else:
                    TT(tmpm[i2].all(), gs[i2][2].all(), PS(yb), ALU.mult)
                    TT(mergedT[c, tsl], acc[i2].all(), tmpm[i2].all(), ALU.add)
            it += 1
    dump("mergedT", mergedT.all())

    for cg in range(4):
        for t in range(2):
            sp_dma(xres[4 * cg:4 * cg + 4, t * 512:(t + 1) * 512], xT_own[t][:, 4 * cg:4 * cg + 4, :], f"xres{cg}{t}")
    sq2 = [B(108 + i, [512], BF16) for i in range(4)]
    r2a = [B(124, [512], F32), B(126, [512], F32)]
    r2b = [B(128, [512], F32), B(130, [512], F32)]
    u6 = 0
    pend = []
    for cg in range(4):
        w = ring_load(wout_d[cg])
        for cc in range(4):
            c = cg * 4 + cc
            for t in range(2):
                tsl = slice(t * 512, (t + 1) * 512)
                pb = u6 % 6
                for k in range(16):
                    MM(PS(pb), w[k, cc * 128:(cc + 1) * 128], mergedT[k, tsl], k == 0, k == 15)
                TT(xres[c, tsl], xres[c, tsl], PS(pb), ALU.add)
                ACT(sq2[u6 % 4].all(), xres[c, tsl], AF.Square)
                pend.append((u6, c, t))
                if len(pend) > 2:
                    uu, c_, t_ = pend.pop(0)
                    MM(PS(6 + t_), ones.all(), sq2[uu % 4].all(), c_ == 0, c_ == 15)
                u6 += 1
    for uu, c_, t_ in pend:
        MM(PS(6 + t_), ones.all(), sq2[uu % 4].all(), c_ == 0, c_ == 15)
    dump("x1", xres.all())

    for t in range(2):
        tsl = slice(t * 512, (t + 1) * 512)
        ACT(r2a[t].all(), PS(6 + t), AF.Sqrt, scale=1.0 / D, bias=eps_col)
        RECIP(r2b[t].all(), r2a[t].all())
        for c in range(16):
            STT(h2T[c, tsl], xres[c, tsl], g_ffn(c), r2b[t].all(), ALU.mult, ALU.mult)

    rsq = [B(124 + 2 * i, [512], F32) for i in range(2)]
    rsq = [B(188 + 2 * i, [512], F32) for i in range(2)]
    ei = 0
    for G in range(4):
        for ub in range(4):
            w = ring_load(wup_d[G * 4 + ub])
            for jj in range(4):
                j = ub * 4 + jj
                for t in range(2):
                    tsl = slice(t * 512, (t + 1) * 512)
                    pb = ei % 2
                    for k in range(16):
                        MM(PS(pb), w[k, jj * 128:(jj + 1) * 128], h2T[k, tsl], k == 0, k == 15)
                    r = rsq[ei % 2].all()
                    ACT(r, PS(pb), AF.Relu)
                    TT(hid[j, tsl], r, r, ALU.mult)
                    ei += 1
        for cg in range(4):
            w = ring_load(wdn_d[G * 4 + cg])
            for cc in range(4):
                c = cg * 4 + cc
                for t in range(2):
                    tsl = slice(t * 512, (t + 1) * 512)
                    pb = 2 + (ei % 2)
                    ei += 1
                    for j in range(16):
                        MM(PS(pb), w[j, cc * 128:(cc + 1) * 128], hid[j, tsl], j == 0, j == 15)
                    TT(xres[c, tsl], xres[c, tsl], PS(pb), ALU.add)
            if G == 3:
                P.add("sp", (lambda cg: lambda e: e.dma_start(out=outT[:, 4 * cg:4 * cg + 4, :],
                                                               in_=xres[4 * cg:4 * cg + 4, :].ap))(cg),
                      reads=[xres[4 * cg:4 * cg + 4, :]], dma_slot=f"out{cg}")
    P.emit()
    st.close()
    return nc, P


def _rope_tables():
    rows = S // 64
    t_row = np.repeat(np.arange(rows), 64).astype(np.float64)
    t_col = np.tile(np.arange(64), rows).astype(np.float64)
    inv = 10000.0 ** (-np.arange(32, dtype=np.float64) / 32.0)
    ar = t_row[:, None] * inv
    ac = t_col[:, None] * inv
    ang = np.concatenate([ar, ar, ac, ac], axis=-1)
    cos = np.cos(ang)
    sin = np.sin(ang)
    sgn = np.ones(128, np.float64)
    sgn[0:32] = -1.0
    sgn[64:96] = -1.0
    return cos.astype(np.float32), (sin * sgn).astype(np.float32)


def _chunked(w, cols):
    K = w.shape[0] // 128
    return np.ascontiguousarray(w.reshape(K, 128, -1).transpose(1, 0, 2))


def _prep_inputs(inputs):
    f = lambda a: np.ascontiguousarray(np.asarray(a, dtype=np.float32))
    x = f(inputs["x"])[0]
    mem = f(inputs["mem"])[0]
    w_in = f(inputs["w_in"])[0]
    cos, sinS = _rope_tables()
    xT = x.reshape(16, 512, 16, 128).transpose(0, 3, 2, 1)
    xT_all = np.ascontiguousarray(xT)
    tabk = lambda t: np.ascontiguousarray(t.reshape(16, 4, 128, 128).transpose(0, 2, 1, 3))
    rep = lambda v: np.ascontiguousarray(np.broadcast_to(v[None, :], (128, v.shape[0])))
    gvec = np.concatenate([f(inputs[k])[0].reshape(16, 128).T for k in ("norm_mix", "mem_norm", "norm_ffn")], axis=1)
    grow = np.stack([rep(f(inputs[k])[0]) for k in ("q_norm", "k_norm", "mq_norm", "mk_norm")], axis=1)
    ws = f(inputs["w_spatial"])[0]
    wsT = np.ascontiguousarray(ws.transpose(2, 0, 1).reshape(128, 512))
    bsp = rep(f(inputs["b_spatial"])[0].reshape(512))
    w_ao = f(inputs["w_attn_o"])[0]
    w_go = f(inputs["w_gmlp_o"])[0]
    w_mo = f(inputs["w_mem_o"])[0]
    wp5 = np.empty((16, 128, 64, 128), np.float32)
    for c in range(16):
        cs = slice(c * 128, (c + 1) * 128)
        for br in range(3):
            wp5[c, :, br * 16:(br + 1) * 16, :] = _chunked(w_in[:, 3072 + br * 2048 + c * 128: 3072 + br * 2048 + (c + 1) * 128], 128)
        wp5[c, :, 48:56, :] = _chunked(w_ao[:, cs], 128)
        wp5[c, :, 56:60, :] = _chunked(w_go[:, cs], 128)
        wp5[c, :, 60:64, :] = _chunked(w_mo[:, cs], 128)
    w_out = f(inputs["w_out"])[0]
    wout = np.stack([_chunked(w_out[:, cg * 512:(cg + 1) * 512], 512) for cg in range(4)])
    w_up = f(inputs["w_ffn_up"])[0]
    wup = np.stack([_chunked(w_up[:, b * 512:(b + 1) * 512], 512) for b in range(16)])
    w_dn = f(inputs["w_ffn_down"])[0]
    wdn = np.stack([_chunked(w_dn[G * 2048:(G + 1) * 2048, cg * 512:(cg + 1) * 512], 512)
                    for G in range(4) for cg in range(4)])
    shared = dict(
        xT_all=xT_all,
        cos_k=tabk(cos), sin_k=tabk(sinS),
        ident=np.eye(128, dtype=np.float32),
        gvec=np.ascontiguousarray(gvec), grow=np.ascontiguousarray(grow),
        sgu=rep(f(inputs["sgu_norm"])[0]), bsp=bsp, wsT=wsT,
        memT=np.ascontiguousarray(mem.reshape(256, 16, 128).transpose(2, 1, 0)),
        wmkv=_chunked(f(inputs["w_mem_kv"])[0], 1024),
        wkv=_chunked(w_in[:, 1024:1536], 512),
        wq=_chunked(w_in[:, 0:1024], 1024),
        wqm=_chunked(w_in[:, 2560:3072], 512),
        wzu=_chunked(w_in[:, 1536:2048], 512),
        wzv=_chunked(w_in[:, 2048:2560], 512),
        wp5=wp5.reshape(16, 128, 4, 2048),
        wout=wout.reshape(4, 128, 4, 2048),
        wup=wup.reshape(16, 128, 4, 2048),
        wdn=wdn.reshape(16, 128, 4, 2048),
    )
    in_maps = []
    for i in range(NCORES):
        m = dict(shared)
        m["xT_own"] = np.ascontiguousarray(xT_all[2 * i:2 * i + 2])
        tq = lambda t: np.ascontiguousarray(t[i * TOK:(i + 1) * TOK].reshape(8, 128, 128).transpose(1, 0, 2))
        m["cos_q"] = tq(cos)
        m["sin_q"] = tq(sinS)
        in_maps.append(m)
    return in_maps


_NC_CACHE = {}


def kernel(**inputs):
    in_maps = _prep_inputs(inputs)
    if "nc" not in _NC_CACHE:
        _NC_CACHE["nc"] = build()[0]
    nc = _NC_CACHE["nc"]
    res = run_bass_kernel_spmd(nc, in_maps, core_ids=list(range(NCORES)))
    outs = []
    for i in range(NCORES):
        o = np.asarray(res.results[i]["outT"])
        outs.append(o.transpose(2, 1, 0).reshape(TOK, D))
    return np.concatenate(outs, axis=0)[None].astype(np.float32)
```
